# Optimizing a Trainium2 kernel written in Bass

```python
import math
import jax
import jax.numpy as jnp
from jax import lax
import numpy as np

D_MODEL = 1024
BATCH = 8
SEQ = 4096
DEPTH = 4

HEAD_DIM = 64
DIL_GROUPS = ((128, 1), (512, 4), (2048, 16))
HEADS_PER_DIL_GROUP = 4
N_HEADS_A = HEADS_PER_DIL_GROUP * len(DIL_GROUPS)
N_HEADS_B = 6
N_HEADS_C = 6
N_HEADS = N_HEADS_A + N_HEADS_B + N_HEADS_C
MIX_WIDTH = N_HEADS * HEAD_DIM
WIDTH_A = HEADS_PER_DIL_GROUP * HEAD_DIM
WIDTH_B = N_HEADS_B * HEAD_DIM
WIDTH_C = N_HEADS_C * HEAD_DIM
ROPE_DIM = HEAD_DIM // 4
ROPE_THETA = 500000.0
MOBA_BLOCK = 256
MOBA_TOPK = 3
MOBA_QCHUNK = 32
SB_QBLOCK = 128
D_FF = 4 * D_MODEL
N_BRANCHES = 3
NORM_EPS = 1e-6
IN_COLS = 3 * MIX_WIDTH + N_BRANCHES * D_MODEL

kernel_name = "hybrid_dilated_moba_stickbreak_block"


def rmsnorm(x, g):
    xf = x.astype(jnp.float32)
    y = xf * lax.rsqrt(jnp.mean(xf * xf, axis=-1, keepdims=True) + NORM_EPS)
    return (y * g.astype(jnp.float32)).astype(x.dtype)


def partial_rope(t, pos):
    inv_freq = jnp.exp(-math.log(ROPE_THETA) * jnp.arange(0, ROPE_DIM, 2, dtype=jnp.float32) / ROPE_DIM)
    ang = pos[:, None] * inv_freq[None, :]
    cos, sin = jnp.cos(ang), jnp.sin(ang)
    half = ROPE_DIM // 2
    x1, x2, rest = t[..., :half], t[..., half:ROPE_DIM], t[..., ROPE_DIM:]
    return jnp.concatenate([x1 * cos - x2 * sin, x2 * cos + x1 * sin, rest], axis=-1)


def dilated_window_attention(q, k, v, window, dilation):
    B, H, S, Dh = q.shape
    span = window // dilation
    blk = span
    unit = dilation * blk
    L = -(-S // unit) * unit
    nb = L // unit
    pad = ((0, 0), (0, 0), (0, L - S), (0, 0))

    def split(t):
        t = jnp.pad(t, pad).reshape(B, H, L // dilation, dilation, Dh)
        return jnp.swapaxes(t, 2, 3).reshape(B, H, dilation, nb, blk, Dh)

    def with_prev(t):
        prev = jnp.concatenate([jnp.zeros_like(t[:, :, :, :1]), t[:, :, :, :-1]], axis=3)
        return jnp.concatenate([prev, t], axis=4)

    qb = split(q)
    kc, vc = with_prev(split(k)), with_prev(split(v))
    s = jnp.einsum('bhrnqd,bhrnkd->bhrnqk', qb, kc) * (Dh ** -0.5)
    qi = jnp.arange(blk)[:, None]
    kj = jnp.arange(2 * blk)[None, :]
    dist = qi + blk - kj
    band = (dist >= 0) & (dist <= span)
    first = (jnp.arange(nb) == 0)[:, None, None]
    mask = band[None] & ~(first & (kj < blk)[None])
    s = jnp.where(mask, s, -jnp.inf)
    m = jnp.max(s, axis=-1, keepdims=True)
    p = jnp.exp(s - m)
    l = jnp.sum(p, axis=-1, keepdims=True)
    o = jnp.einsum('bhrnqk,bhrnkd->bhrnqd', p, vc) / l
    lse = (m + jnp.log(l))[..., 0]

    def merge(t):
        tail = t.shape[5:]
        t = t.reshape((B, H, dilation, L // dilation) + tail)
        return jnp.swapaxes(t, 2, 3).reshape((B, H, L) + tail)[:, :, :S]

    return merge(o), merge(lse)


def dilated_mixture(q, k, v):
    outs, lses = [], []
    for g, (w, d) in enumerate(DIL_GROUPS):
        sl = slice(g * HEADS_PER_DIL_GROUP, (g + 1) * HEADS_PER_DIL_GROUP)
        o, lse = dilated_window_attention(q[:, sl], k[:, sl], v[:, sl], w, d)
        outs.append(o)
        lses.append(lse)
    o = jnp.stack(outs, axis=0)
    wgt = jax.nn.softmax(jnp.stack(lses, axis=0), axis=0)
    return jnp.sum(wgt[..., None] * o, axis=0)


def moba_attention(q, k, v):
    B, H, S, Dh = q.shape
    L = -(-S // MOBA_BLOCK) * MOBA_BLOCK
    nblk = L // MOBA_BLOCK
    topk = min(MOBA_TOPK, nblk)
    pad = ((0, 0), (0, 0), (0, L - S), (0, 0))
    qp, kp, vp = jnp.pad(q, pad), jnp.pad(k, pad), jnp.pad(v, pad)
    kblocks = kp.reshape(B, H, nblk, MOBA_BLOCK, Dh)
    vblocks = vp.reshape(B, H, nblk, MOBA_BLOCK, Dh)
    kmean = jnp.mean(kblocks, axis=3)
    scale = Dh ** -0.5
    gather = jax.vmap(jax.vmap(lambda blocks, idx: blocks[idx]))

    def chunk(c):
        start = c * MOBA_QCHUNK
        qc = lax.dynamic_slice_in_dim(qp, start, MOBA_QCHUNK, axis=2)
        qpos = start + jnp.arange(MOBA_QCHUNK)
        own = start // MOBA_BLOCK
        gate = jnp.einsum('bhqd,bhnd->bhqn', qc, kmean)
        gate = jnp.where((jnp.arange(nblk) < own)[None, None, None, :], gate, -jnp.inf)
        _, idx = lax.top_k(gate, topk)
        valid = idx < own
        ksel = gather(kblocks, idx)
        vsel = gather(vblocks, idx)
        s_sel = jnp.einsum('bhqd,bhqjkd->bhqjk', qc, ksel) * scale
        s_sel = jnp.where(valid[..., None], s_sel, -jnp.inf).reshape(B, H, MOBA_QCHUNK, topk * MOBA_BLOCK)
        kown = lax.dynamic_index_in_dim(kblocks, own, axis=2, keepdims=False)
        vown = lax.dynamic_index_in_dim(vblocks, own, axis=2, keepdims=False)
        s_own = jnp.einsum('bhqd,bhkd->bhqk', qc, kown) * scale
        kpos = own * MOBA_BLOCK + jnp.arange(MOBA_BLOCK)
        s_own = jnp.where(kpos[None, :] <= qpos[:, None], s_own, -jnp.inf)
        p = jax.nn.softmax(jnp.concatenate([s_own, s_sel], axis=-1), axis=-1)
        p_own = p[..., :MOBA_BLOCK]
        p_sel = p[..., MOBA_BLOCK:].reshape(B, H, MOBA_QCHUNK, topk, MOBA_BLOCK)
        return (jnp.einsum('bhqk,bhkd->bhqd', p_own, vown)
                + jnp.einsum('bhqjk,bhqjkd->bhqd', p_sel, vsel))

    o = lax.map(chunk, jnp.arange(L // MOBA_QCHUNK))
    return jnp.moveaxis(o, 0, 2).reshape(B, H, L, Dh)[:, :, :S]


def stick_breaking_attention(q, k, v):
    B, H, S, Dh = q.shape
    scale = Dh ** -0.5
    outs = []
    for start in range(0, S, SB_QBLOCK):
        end = min(start + SB_QBLOCK, S)
        z = jnp.einsum('bhqd,bhkd->bhqk', q[:, :, start:end], k[:, :, :end]) * scale
        causal = jnp.arange(end)[None, :] < jnp.arange(start, end)[:, None]
        log_1m = jnp.where(causal, jax.nn.log_sigmoid(-z), 0.0)
        suffix = lax.cumsum(log_1m, axis=3, reverse=True) - log_1m
        a = jnp.where(causal, jnp.exp(jax.nn.log_sigmoid(z) + suffix), 0.0)
        outs.append(jnp.einsum('bhqk,bhkd->bhqd', a, v[:, :, :end]))
    return jnp.concatenate(outs, axis=2)


def hybrid_layer(x, norm_mix, w_in, w_out_a, w_out_b, w_out_c, w_o, norm_mlp, w_up, w_down, pos):
    B, S, D = x.shape
    h = rmsnorm(x, norm_mix)
    proj = h @ w_in

    def heads(t):
        return t.reshape(B, S, N_HEADS, HEAD_DIM).transpose(0, 2, 1, 3).astype(jnp.float32)

    q = heads(proj[..., :MIX_WIDTH])
    k = heads(proj[..., MIX_WIDTH:2 * MIX_WIDTH])
    v = heads(proj[..., 2 * MIX_WIDTH:3 * MIX_WIDTH])
    gates = jax.nn.sigmoid(proj[..., 3 * MIX_WIDTH:]).reshape(B, S, N_BRANCHES, D)

    n_rot = N_HEADS_A + N_HEADS_B
    q = jnp.concatenate([partial_rope(q[:, :n_rot], pos), q[:, n_rot:]], axis=1)
    k = jnp.concatenate([partial_rope(k[:, :n_rot], pos), k[:, n_rot:]], axis=1)

    a_sl = slice(0, N_HEADS_A)
    b_sl = slice(N_HEADS_A, N_HEADS_A + N_HEADS_B)
    c_sl = slice(N_HEADS_A + N_HEADS_B, N_HEADS)
    o_a = dilated_mixture(q[:, a_sl], k[:, a_sl], v[:, a_sl])
    o_b = moba_attention(q[:, b_sl], k[:, b_sl], v[:, b_sl])
    o_c = stick_breaking_attention(q[:, c_sl], k[:, c_sl], v[:, c_sl])

    def flat(o):
        return o.transpose(0, 2, 1, 3).reshape(B, S, -1).astype(x.dtype)

    y_a = flat(o_a) @ w_out_a
    y_b = flat(o_b) @ w_out_b
    y_c = flat(o_c) @ w_out_c
    merged = gates[:, :, 0] * y_a + gates[:, :, 1] * y_b + gates[:, :, 2] * y_c
    x = x + merged @ w_o

    h = rmsnorm(x, norm_mlp)
    return x + jnp.square(jax.nn.relu(h @ w_up)) @ w_down


def setup_inputs(seed: int = 0) -> dict:
    key = jax.random.key(seed)
    ks = jax.random.split(key, 12)
    f32 = jnp.float32

    def nrm(k, shape, fan_in):
        return jax.random.normal(k, shape, f32) * (fan_in ** -0.5)

    x = jax.random.normal(ks[0], (BATCH, SEQ, D_MODEL), f32)
    norm_mix = 1.0 + 0.05 * jax.random.normal(ks[1], (DEPTH, D_MODEL), f32)
    w_in = nrm(ks[2], (DEPTH, D_MODEL, IN_COLS), D_MODEL)
    w_out_a = nrm(ks[3], (DEPTH, WIDTH_A, D_MODEL), WIDTH_A)
    w_out_b = nrm(ks[4], (DEPTH, WIDTH_B, D_MODEL), WIDTH_B)
    w_out_c = nrm(ks[5], (DEPTH, WIDTH_C, D_MODEL), WIDTH_C)
    w_o = nrm(ks[6], (DEPTH, D_MODEL, D_MODEL), D_MODEL)
    norm_mlp = 1.0 + 0.05 * jax.random.normal(ks[7], (DEPTH, D_MODEL), f32)
    w_up = nrm(ks[8], (DEPTH, D_MODEL, D_FF), D_MODEL)
    w_down = nrm(ks[9], (DEPTH, D_FF, D_MODEL), D_FF)
    norm_final = 1.0 + 0.05 * jax.random.normal(ks[10], (D_MODEL,), f32)
    return {"x": x, "norm_mix": norm_mix, "w_in": w_in, "w_out_a": w_out_a,
            "w_out_b": w_out_b, "w_out_c": w_out_c, "w_o": w_o, "norm_mlp": norm_mlp,
            "w_up": w_up, "w_down": w_down, "norm_final": norm_final}


def reference(x, norm_mix, w_in, w_out_a, w_out_b, w_out_c, w_o, norm_mlp, w_up, w_down, norm_final):
    pos = jnp.arange(x.shape[1], dtype=jnp.float32)
    for layer in range(DEPTH):
        x = hybrid_layer(x, norm_mix[layer], w_in[layer], w_out_a[layer], w_out_b[layer],
                         w_out_c[layer], w_o[layer], norm_mlp[layer], w_up[layer],
                         w_down[layer], pos)
    return rmsnorm(x, norm_final)
```

```python
import contextlib
import numpy as np
import concourse.bass as bass
import concourse.mybir as mybir
from concourse.bass_utils import run_bass_kernel_spmd

F32 = mybir.dt.float32
BF16 = mybir.dt.bfloat16
AF = mybir.ActivationFunctionType
ALU = mybir.AluOpType
AX = mybir.AxisListType

D = 1024
S = 4096
DEPTH = 4
NCH = 8
TG = 512
NTG = S // TG
NT = S // 128
DFF = 4096
NEG = -30000.0
SB_WIN = 3
EPS = 1e-6
DIL = (1, 4, 16)

ENGS = ("pe", "act", "dve", "pool", "sp")
SAME_ENGINE_SYNC = {"pe": False, "act": True, "dve": True, "pool": True, "sp": False}
NSLOT = 16
SEM_WRAP = 30000


class Buf:
    __slots__ = ("name", "last_w", "readers", "excl")

    def __init__(self, name, excl=False):
        self.name = name
        self.last_w = None
        self.readers = []
        self.excl = excl


class Op:
    __slots__ = ("eng", "fn", "deps", "is_dma", "signal", "needed")

    def __init__(self, eng, fn, is_dma):
        self.eng = eng
        self.fn = fn
        self.deps = []
        self.is_dma = is_dma
        self.signal = None
        self.needed = False


class Prog:
    def __init__(self, nc):
        self.nc = nc
        self.ops = {e: [] for e in ENGS}
        self.dmas = {e: [] for e in ENGS}

    def buf(self, name="b"):
        return Buf(name)

    def op(self, eng, fn, r=(), w=(), dma=False, after=()):
        o = Op(eng, fn, dma)
        if STOPPED[0]:
            return o
        deps = list(after)
        if any(b.excl for b in r):
            w = list(w) + [b for b in r if b.excl and b not in w]
            r = [b for b in r if not b.excl]
        for b in r:
            if b.last_w is not None:
                deps.append(b.last_w)
        for b in w:
            if b.last_w is not None:
                deps.append(b.last_w)
            deps.extend(b.readers)
        for b in r:
            b.readers.append(o)
        for b in w:
            b.last_w = o
            b.readers = []
        seen = set()
        for d in deps:
            if d is o or id(d) in seen:
                continue
            if d.eng == eng and not d.is_dma and not SAME_ENGINE_SYNC[eng]:
                continue
            seen.add(id(d))
            o.deps.append(d)
            d.needed = True
        self.ops[eng].append(o)
        if dma:
            self.dmas[eng].append(o)
        return o

    def dma(self, eng, out, in_, r=(), w=()):
        return self.op(eng, lambda e: e.dma_start(out=out, in_=in_), r=r, w=w, dma=True)

    def barrier(self):
        engs = [e for e in ENGS if e != "pool"]
        lasts = []
        for e in engs:
            if self.ops[e]:
                lasts.append(self.ops[e][-1])
            lasts.extend(self.dmas[e][-NSLOT:])
        for e in engs:
            self.op(e, lambda eng: eng.nop(), after=lasts)

    def emit(self, final_wait_ops=()):
        nc = self.nc
        with contextlib.ExitStack() as st:
            sem_cache = {}

            def get_sem(key):
                if key not in sem_cache:
                    sem_cache[key] = st.enter_context(nc.semaphore("s_%s" % "_".join(map(str, key))))
                return sem_cache[key]

            for f in final_wait_ops:
                f.needed = True
            for e in ENGS:
                cnt = 0
                gen = 0
                dcount = 0
                slot_prev = {}
                for o in self.ops[e]:
                    if o.is_dma:
                        slot = dcount % NSLOT
                        use = dcount // NSLOT + 1
                        dcount += 1
                        sem = get_sem((e, "d", slot))
                        o.signal = (sem, 16 * use, 16)
                        prev = slot_prev.get(slot)
                        if prev is not None:
                            o.deps.append(prev)
                        slot_prev[slot] = o
                    elif o.needed:
                        cnt += 1
                        if cnt > SEM_WRAP:
                            gen += 1
                            cnt = 1
                        sem = get_sem((e, "c", gen))
                        o.signal = (sem, cnt, 1)
            engmap = {"pe": "tensor", "act": "scalar", "dve": "vector", "pool": "gpsimd", "sp": "sync"}
            with nc.Block() as block:
                for e in ENGS:
                    ops = self.ops[e]
                    extra = list(final_wait_ops) if e == "sp" else []

                    def body(eng, ops=ops, extra=extra):
                        waited = {}

                        def do_wait(d):
                            sem, val, _ = d.signal
                            k = id(sem)
                            if waited.get(k, 0) >= val:
                                return
                            waited[k] = val
                            eng.wait_ge(sem, val)

                        for o in ops:
                            for d in o.deps:
                                do_wait(d)
                            if isinstance(o.fn, tuple):
                                ins = getattr(eng, o.fn[0])(*o.fn[1], **o.fn[2])
                            else:
                                ins = o.fn(eng)
                            if o.signal is not None:
                                ins.then_inc(o.signal[0], o.signal[2])
                        for f in extra:
                            do_wait(f)

                    getattr(block, engmap[e])(body)


def CALL(name, *a, **k):
    return (name, a, k)


import os
SUB = int(os.environ.get("MK_SUB", "0"))
STOPPED = [False]


def chk(n):
    if SUB == n:
        STOPPED[0] = True


class Rot:
    def __init__(self, items):
        self.items = list(items)
        self.i = 0

    def next(self):
        it = self.items[self.i % len(self.items)]
        self.i += 1
        return it


class WStream:
    def __init__(self, P, eng, bufs, loads, ahead):
        self.P, self.eng, self.bufs, self.loads = P, eng, bufs, loads
        self.ahead = min(ahead, len(bufs) - 1)
        self.issued = 0

    def _issue(self, k):
        t, tok = self.bufs[k % len(self.bufs)]
        out_ap, in_ap, rt = self.loads[k](t)
        self.P.dma(self.eng, out_ap, in_ap, r=rt, w=[tok])

    def get(self, i):
        lim = min(i + self.ahead, len(self.loads) - 1)
        while self.issued <= lim:
            self._issue(self.issued)
            self.issued += 1
        return self.bufs[i % len(self.bufs)]


def build(n_layers=DEPTH, debug=False, final_norm=True, stop_at=99):
    nc = bass.Bass("TRN2", target_bir_lowering=False)
    L = n_layers
    xT_in = nc.dram_tensor("xT", [D, S], F32, kind="ExternalInput").ap()
    wqkv = nc.dram_tensor("wqkv", [L, 36, 128, NCH * 128], F32, kind="ExternalInput").ap()
    wgate = nc.dram_tensor("wgate", [L, 8, 128, NCH * 3 * 128], F32, kind="ExternalInput").ap()
    wout = nc.dram_tensor("wout", [L, 8, 128, NCH * 128], F32, kind="ExternalInput").ap()
    wo = nc.dram_tensor("wo", [L, 8, 128, NCH * 128], F32, kind="ExternalInput").ap()
    wup = nc.dram_tensor("wup", [L, 16, 128, NCH * 256], F32, kind="ExternalInput").ap()
    wdn = nc.dram_tensor("wdn", [L, 16, 128, 2 * 1024], F32, kind="ExternalInput").ap()
    gains_d = nc.dram_tensor("gains", [128, DEPTH * 16 + 8 + 128], F32, kind="ExternalInput").ap()
    cf_d = nc.dram_tensor("cf", [128, 2, S], F32, kind="ExternalInput").ap()
    cb_d = nc.dram_tensor("cb", [128, 1408], F32, kind="ExternalInput").ap()
    ind_d = nc.dram_tensor("ind", [16, S], F32, kind="ExternalInput").ap()
    outT = nc.dram_tensor("outT", [D, S], F32, kind="ExternalOutput").ap()
    xs = nc.dram_tensor("xs", [D, S], F32).ap()
    wqkv_b = nc.dram_tensor("wqkv_b", [L, 36, 128, NCH * 128], BF16).ap()
    wgate_b = nc.dram_tensor("wgate_b", [L, 8, 128, NCH * 3 * 128], BF16).ap()
    wout_b = nc.dram_tensor("wout_b", [L, 8, 128, NCH * 128], BF16).ap()
    wo_b = nc.dram_tensor("wo_b", [L, 8, 128, NCH * 128], BF16).ap()
    wup_b = nc.dram_tensor("wup_b", [L, 16, 128, NCH * 256], BF16).ap()
    wdn_b = nc.dram_tensor("wdn_b", [L, 16, 128, 2 * 1024], BF16).ap()
    if debug:
        oT = nc.dram_tensor("oT", [D, S], BF16, kind="ExternalOutput").ap()
    else:
        oT = nc.dram_tensor("oT", [D, S], BF16).ap()

    P = Prog(nc)
    st = contextlib.ExitStack()
    with st:
        cur_st = [st]
        uid = [0]

        def sb(name, shape, dt):
            uid[0] += 1
            return cur_st[0].enter_context(nc.sbuf_tensor("%s_%d" % (name, uid[0]), shape, dt)), P.buf(name)

        def sbn(name, shape, dt, n):
            return [sb("%s%d" % (name, i), shape, dt) for i in range(n)]

        banks = []
        for i in range(8):
            banks.append((st.enter_context(nc.psum_tensor("pb%d" % i, [128, 512], F32)), Buf("pb%d" % i, excl=True)))

        hT, _ = sb("hT", [128, NCH, S], BF16)
        hT_tok = [P.buf("hT%d" % g) for g in range(NTG)]
        cbt, cb_tok = sb("cbt", [128, 1408], BF16)
        ident = cbt[:, 0:128]
        perm = cbt[:, 128:256]
        maskA = cbt[:, 256:512]
        maskLE = cbt[:, 384:512]
        maskS = cbt[:, 512:640]
        uincl = cbt[:, 640:768]
        negones = cbt[:, 768:896]
        m01A = cbt[:, 896:1152]
        m01S = cbt[:, 1152:1280]
        onesb = cbt[:, 1280:1408]
        gains, gains_tok = sb("gains", [128, DEPTH * 16 + 8 + 128], F32)
        cf_tok = gains_tok
        ones32 = gains[:, DEPTH * 16 + 8:DEPTH * 16 + 8 + 128]
        sq_l = sbn("sq", [128, TG], BF16, 3)
        rstd_l = sbn("rstd", [128, TG], F32, 2)
        t2_l = sbn("t2", [128, TG], F32, 2)

        P.dma("sp", gains[:, :], gains_d[:, :], w=[gains_tok])
        P.dma("pool", cbt[:, :], cb_d[:, :], w=[cb_tok])

        units = []
        for ap_ in range(2):
            units.append(("A", ap_))
        for bp in range(3):
            units.append(("B", bp))
        for cp in range(3):
            units.append(("C", cp))
        blk_order = []
        for kind, idx in units:
            if kind == "A":
                for gi in range(3):
                    b = 2 * gi + idx
                    blk_order += [b, 12 + b, 24 + b]
            elif kind == "B":
                blk_order += [6 + idx, 18 + idx, 30 + idx]
            else:
                blk_order += [9 + idx, 21 + idx, 33 + idx]
        wtok = {}

        def conv_layer(l, part, after=()):
            aft = [list(after)]

            def cv(key, dst, src):
                wtok[key] = P.buf("w%s" % (key,))
                o_ = P.op("pool", lambda e, dst=dst, src=src: e.dma_start(out=dst, in_=src), w=[wtok[key]], dma=True, after=aft[0])
                aft[0] = [o_] if (l > 0 or part == 1) else []
            if part == 0:
                for b in blk_order:
                    cv(("qkv", l, b), wqkv_b[l, b], wqkv[l, b])
                return
            for cb_ in range(8):
                cv(("gate", l, cb_), wgate_b[l, cb_], wgate[l, cb_])
                cv(("out", l, cb_), wout_b[l, cb_], wout[l, cb_])
            for cb_ in range(8):
                cv(("o", l, cb_), wo_b[l, cb_], wo[l, cb_])
            for j in range(16):
                cv(("up", l, j), wup_b[l, j], wup[l, j])
            for j in range(16):
                cv(("dn", l, j), wdn_b[l, j], wdn[l, j])

        conv_layer(0, 0)
        xs_tok = [P.buf("xs%d" % g) for g in range(NTG)]
        oT_tok = [P.buf("oT%d" % u) for u in range(8)]
        out_dmas = []

        def gain_ap(l, which, c):
            col = l * 16 + which * 8 + c
            return gains[:, col:col + 1]

        def gain_final(c):
            col = DEPTH * 16 + c
            return gains[:, col:col + 1]

        sq_rot = Rot(sq_l)
        rstd_rot = Rot(rstd_l)

        def norm_stat(xg, xg_tok, bank, c):
            bk, bk_tok = bank
            sq, sq_tok = sq_rot.next()
            P.op("act", CALL("activation", out=sq[:, :], in_=xg[:, c, :], func=AF.Square), r=[xg_tok], w=[sq_tok])
            P.op("pe", CALL("matmul", bk[:, :], lhsT=onesb, rhs=sq[:, :], start=(c == 0), stop=(c == NCH - 1)),
                 r=[sq_tok, cb_tok], w=[bk_tok])

        def norm_finish(xg, xg_tok, bank, gain_fn, out_fn, out_toks, per_chunk=None, after_chunk=None):
            bk, bk_tok = bank
            rstd, rstd_tok = rstd_rot.next()
            P.op("act", CALL("activation", out=rstd[:, :], in_=bk[:, :], func=AF.Ln, bias=EPS, scale=1.0),
                 r=[bk_tok], w=[rstd_tok])
            P.op("act", CALL("activation", out=rstd[:, :], in_=rstd[:, :], func=AF.Exp, scale=-0.5),
                 r=[rstd_tok], w=[rstd_tok])
            for c in range(NCH):
                oap = out_fn(c)
                wt = list(out_toks) if per_chunk is None else [per_chunk(c)]
                P.op("dve", CALL("scalar_tensor_tensor", out=oap, in0=xg[:, c, :], scalar=gain_fn(c), in1=rstd[:, :],
                                 op0=ALU.mult, op1=ALU.mult),
                     r=[xg_tok, rstd_tok, gains_tok], w=wt)
                if after_chunk is not None:
                    after_chunk(c)

        def norm_group(xg, xg_tok, bank, gain_fn, out_fn, out_toks, per_chunk=None, after_chunk=None):
            for c in range(NCH):
                norm_stat(xg, xg_tok, bank, c)
            norm_finish(xg, xg_tok, bank, gain_fn, out_fn, out_toks, per_chunk, after_chunk)

        def x_src_ap(l, g):
            src = xT_in if l == 0 else xs
            return src.rearrange("(c p) t -> p c t", p=128)[:, :, g * TG:(g + 1) * TG]

        for l in range(L):
            P.barrier()
            ph = contextlib.ExitStack()
            cur_st[0] = ph
            if l == 0:
                xg_l = sbn("xg", [128, NCH, TG], F32, 2)
                p1banks = Rot(banks[0:2])
                for g in range(NTG):
                    xg, xg_tok = xg_l[g % 2]
                    P.dma("sp", xg[:, :, :], x_src_ap(l, g), r=([xs_tok[g]] if l > 0 else []), w=[xg_tok])
                    norm_group(xg, xg_tok, p1banks.next(), lambda c: gain_ap(l, 0, c),
                               lambda c, g=g: hT[:, c, g * TG:(g + 1) * TG], [hT_tok[g]])
                P.barrier()
            ph.close()
            if stop_at <= 1:
                break
            ph = contextlib.ExitStack()
            cur_st[0] = ph
            wq_l = sbn("wq", [128, NCH, 128], BF16, 4)
            QK_l = [(sb("QT%d" % i, [128, S], BF16), sb("KT%d" % i, [128, S], BF16)) for i in range(2)]
            Vb, Vb_tok = sb("Vb", [128, NT * 192], BF16)
            acc_l = sbn("acc", [128, S], F32, 2)
            oTp_l = sbn("oTp", [128, S], BF16, 1)
            qraw_l = sbn("qraw", [128, TG], BF16, 2)
            t1_l = sbn("t1", [128, TG], F32, 2)
            cs_l = sbn("cs", [128, 2, TG], F32, 2)
            PT_l = sbn("PT", [128, 512], BF16, 3)
            E_l = sbn("E", [128, 384], F32, 3)
            Ec_l = sbn("Ec", [128, 384], F32, 2)
            Lp_l = sbn("Lp", [128, 384], BF16, 2)
            Aw_l = sbn("Aw", [128, 384], BF16, 2)
            km32, km32_tok = sb("km32", [128, 16], F32)
            kmh, kmh_tok = sb("kmh", [128, 16], BF16)
            kml, kml_tok = sb("kml", [128, 16], BF16)
            gate_l = sbn("gate", [128, 16], F32, 2)
            m8_l = sbn("m8", [128, 8], F32, 2)
            sel_l = sbn("sel", [128, 16], F32, 2)
            nm_l = sbn("nm", [128, 16], BF16, 24)
            KZ_l = sbn("KZ", [128, S], BF16, 1)

            wq_stream = WStream(P, "sp", wq_l,
                                [(lambda t, b=b: (t[:, :, :], wqkv_b[l, b].rearrange("p (c n) -> p c n", n=128), [wtok[("qkv", l, b)]])) for b in blk_order],
                                ahead=3)
            wq_i = [0]

            def next_wq():
                r_ = wq_stream.get(wq_i[0])
                wq_i[0] += 1
                return r_

            projb = Rot(banks[0:3])
            projb2 = Rot(banks[3:5])
            bg_tasks = []

            def run_bg(n=1):
                for _ in range(n):
                    if bg_tasks:
                        bg_tasks.pop(0)()
            qraw_rot, t1_rot, t2_rot, cs_rot = Rot(qraw_l), Rot(t1_l), Rot(t2_l), Rot(cs_l)
            evac_flip = [0]

            def proj_fm(dst, dst_tok, rope):
                if os.environ.get("MK_NOROPE"):
                    rope = False
                w, w_tok = next_wq()
                pending = []
                cs_q = []

                def cs_load(g_):
                    cs, cs_tok = cs_rot.next()
                    P.dma("sp", cs[:, :, :], cf_d[:, :, g_ * TG:(g_ + 1) * TG], w=[cs_tok])
                    cs_q.append((cs, cs_tok))
                if rope:
                    cs_load(0)

                def rope_tail(bk, bk_tok, qr, qr_tok, cols):
                    bk2, bk2_tok = projb2.next()
                    P.op("pe", CALL("matmul", bk2[:, :], lhsT=perm, rhs=qr[:, :], start=True, stop=True),
                         r=[qr_tok, cb_tok], w=[bk2_tok])
                    t1, t1_tok = t1_rot.next()
                    t2, t2_tok = t2_rot.next()
                    cs, cs_tok = cs_q.pop(0)
                    gnext = cols.stop // TG
                    if gnext < NTG:
                        cs_load(gnext)
                    P.op("dve", CALL("tensor_tensor", out=t1[:, :], in0=bk[:, :], in1=cs[:, 0, :], op=ALU.mult),
                         r=[bk_tok, cs_tok], w=[t1_tok])
                    P.op("dve", CALL("tensor_tensor", out=t2[:, :], in0=bk2[:, :], in1=cs[:, 1, :], op=ALU.mult),
                         r=[bk2_tok, cs_tok], w=[t2_tok])
                    P.op("dve", CALL("tensor_tensor", out=dst[:, cols], in0=t1[:, :], in1=t2[:, :], op=ALU.add),
                         r=[t1_tok, t2_tok], w=[dst_tok])

                for g in range(NTG):
                    bk, bk_tok = projb.next()
                    cols = slice(g * TG, (g + 1) * TG)
                    for c in range(NCH):
                        P.op("pe", CALL("matmul", bk[:, :], lhsT=w[:, c, :], rhs=hT[:, c, cols],
                                        start=(c == 0), stop=(c == NCH - 1)),
                             r=[w_tok, hT_tok[g]], w=[bk_tok])
                    if not rope:
                        evac_flip[0] ^= 1
                        if evac_flip[0]:
                            P.op("act", CALL("activation", out=dst[:, cols], in_=bk[:, :], func=AF.Copy),
                                 r=[bk_tok], w=[dst_tok])
                        else:
                            P.op("dve", CALL("tensor_copy", out=dst[:, cols], in_=bk[:, :]),
                                 r=[bk_tok], w=[dst_tok])
                    else:
                        qr, qr_tok = qraw_rot.next()
                        P.op("act", CALL("activation", out=qr[:, :], in_=bk[:, :], func=AF.Copy),
                             r=[bk_tok], w=[qr_tok])
                        if pending:
                            rope_tail(*pending.pop(0))
                        pending.append((bk, bk_tok, qr, qr_tok, cols))
                    run_bg(1)
                while pending:
                    rope_tail(*pending.pop(0))

            def proj_v(tile_tok_slices, out_view_fn, in_view_fn):
                w, w_tok = next_wq()
                for quad in range(NT // 4):
                    bk, bk_tok = projb.next()
                    for j in range(4):
                        t = quad * 4 + j
                        gs, sl = tile_tok_slices[t]
                        for c in range(NCH):
                            P.op("pe", CALL("matmul", bk[:, j * 128:(j + 1) * 128], lhsT=hT[:, c, sl], rhs=w[:, c, :],
                                                                                 start=(c == 0), stop=(c == NCH - 1)),
                                 r=[w_tok] + [hT_tok[g] for g in gs], w=[bk_tok])
                    evac_flip[0] ^= 1
                    if evac_flip[0]:
                        P.op("act", CALL("activation", out=out_view_fn(quad), in_=in_view_fn(bk), func=AF.Copy),
                             r=[bk_tok], w=[Vb_tok])
                    else:
                        P.op("dve", CALL("tensor_copy", out=out_view_fn(quad), in_=in_view_fn(bk)),
                             r=[bk_tok], w=[Vb_tok])
                    run_bg(1)

            def tok_slice(d, r, n):
                start = r + d * 128 * n
                return slice(start, start + 127 * d + 1, d)

            def groups_of(sl):
                return sorted(set([sl.start // TG, (sl.stop - 1) // TG]) | set(range(sl.start // TG, (sl.stop - 1) // TG + 1)))

            PT_rot = Rot(PT_l)
            qk_i = 0
            oTp, oTp_tok = oTp_l[0]

            for ui, (kind, idx) in enumerate(units):
                if stop_at <= ui + 1:
                    break
                if kind == "A":
                    VpA = Vb[:, :].rearrange("p (t n) -> p t n", n=192)
                    if idx == 0:
                        P.op("dve", CALL("memset", VpA[:, :, 64:128], 1.0), w=[Vb_tok])
                    Sb_rot = Rot(banks[3:6])
                    Ob_rot = Rot(banks[6:8])
                    for gi, d in enumerate(DIL):
                        (QT, QT_tok), (KT, KT_tok) = QK_l[qk_i % 2]
                        qk_i += 1
                        proj_fm(QT, QT_tok, True)
                        chk(1)
                        proj_fm(KT, KT_tok, True)
                        chk(2)
                        nb = NT // d
                        tiles = []
                        for r in range(d):
                            for n in range(nb):
                                sl = tok_slice(d, r, n)
                                tiles.append((groups_of(sl), sl))
                        proj_v(tiles,
                               lambda quad: VpA[:, quad * 4:(quad + 1) * 4, :].rearrange("p t (h c) -> p t h c", c=64)[:, :, 0:3:2, :],
                               lambda bk: bk[:, :].rearrange("p (t h c) -> p t h c", t=4, h=2))
                        chk(3)
                        run_bg(100)
                        quads = []
                        if d == 1:
                            for m in range(8):
                                quads.append(([(0, 4 * m + j) for j in range(4)],
                                              lambda a, m=m: a[:, 512 * m:512 * (m + 1)].rearrange("p (b i) -> p b i", i=128)))
                        elif d == 4:
                            for n in range(8):
                                quads.append(([(r, n) for r in range(4)],
                                              lambda a, n=n: a[:, 512 * n:512 * (n + 1)].rearrange("p (i r) -> p r i", r=4)))
                        else:
                            for n in range(2):
                                for a4 in range(4):
                                    quads.append(([(4 * a4 + j, n) for j in range(4)],
                                                  lambda a, n=n, a4=a4: a[:, 2048 * n:2048 * (n + 1)].rearrange("p (i r) -> p r i", r=16)[:, 4 * a4:4 * a4 + 4, :]))
                        for hh in range(2):
                            hp = slice(hh * 64, (hh + 1) * 64)
                            acc, acc_tok = acc_l[hh]
                            jobs = []
                            for blocks, accview in quads:
                                ob, ob_tok = Ob_rot.next()
                                for j, (r, n) in enumerate(blocks):
                                    jobs.append((r, n, j, ob, ob_tok, accview if j == 3 else None))
                            pipe = [None, None]
                            for job in jobs + [None, None]:
                                cur = None
                                prev = pipe[0]
                                if job is not None:
                                    r, n, j, ob, ob_tok, accview = job
                                    sbk, sbk_tok = Sb_rot.next()
                                    qsl = tok_slice(d, r, n)
                                    c0 = 0 if n > 0 else 128
                                    if n > 0:
                                        psl = tok_slice(d, r, n - 1)
                                        P.op("pe", CALL("matmul", sbk[:, 0:128], lhsT=KT[hp, psl], rhs=QT[hp, qsl], start=True, stop=True),
                                             r=[KT_tok, QT_tok], w=[sbk_tok])
                                    P.op("pe", CALL("matmul", sbk[:, 128:256], lhsT=KT[hp, qsl], rhs=QT[hp, qsl], start=True, stop=True),
                                         r=[KT_tok, QT_tok], w=[sbk_tok])
                                    pt, pt_tok = PT_rot.next()
                                    P.op("act", CALL("activation", out=pt[:, c0:256], in_=sbk[:, c0:256], func=AF.Exp, scale=0.125),
                                         r=[sbk_tok], w=[pt_tok])
                                    P.op("dve", CALL("tensor_tensor", out=pt[:, c0:256], in0=pt[:, c0:256], in1=m01A[:, c0:256], op=ALU.mult),
                                         r=[pt_tok, cb_tok], w=[pt_tok])
                                    cur = (r, n, j, ob, ob_tok, accview, pt, pt_tok)
                                if prev is not None:
                                    r, n, j, ob, ob_tok, accview, pt, pt_tok = prev
                                    tcur = r * nb + n
                                    vc = slice(hh * 64, hh * 64 + 128)
                                    if n > 0:
                                        P.op("pe", CALL("matmul", ob[:, j * 128:(j + 1) * 128], lhsT=VpA[:, tcur - 1, vc], rhs=pt[:, 0:128], start=True, stop=False),
                                             r=[Vb_tok, pt_tok], w=[ob_tok])
                                    P.op("pe", CALL("matmul", ob[:, j * 128:(j + 1) * 128], lhsT=VpA[:, tcur, vc], rhs=pt[:, 128:256], start=(n == 0), stop=True),
                                         r=[Vb_tok, pt_tok], w=[ob_tok])
                                    if accview is not None:
                                        obv = ob[:, :].rearrange("p (b i) -> p b i", i=128)
                                        if gi == 0:
                                            P.op("dve", CALL("tensor_copy", out=accview(acc), in_=obv),
                                                 r=[ob_tok], w=[acc_tok])
                                        else:
                                            P.op("dve", CALL("tensor_tensor", out=accview(acc), in0=obv, in1=accview(acc), op=ALU.add),
                                                 r=[ob_tok, acc_tok], w=[acc_tok])
                                pipe = [pipe[1], cur]
                            chk(4 + hh + 2 * gi)
                    for hh in range(2):
                        for g in range(NTG):
                            def fin(hh=hh, g=g):
                                acc, acc_tok = acc_l[hh]
                                cols = slice(g * TG, (g + 1) * TG)
                                t1, t1_tok = t1_rot.next()
                                nump = slice(hh * 64, (hh + 1) * 64)
                                denp = slice((1 - hh) * 64, (2 - hh) * 64)
                                P.op("act", CALL("activation", out=t1[nump, :], in_=acc[denp, cols], func=AF.Ln),
                                     r=[acc_tok], w=[t1_tok])
                                P.op("act", CALL("activation", out=t1[nump, :], in_=t1[nump, :], func=AF.Exp, scale=-1.0),
                                     r=[t1_tok], w=[t1_tok])
                                P.op("dve", CALL("tensor_tensor", out=oTp[nump, cols], in0=acc[nump, cols], in1=t1[nump, :], op=ALU.mult),
                                     r=[acc_tok, t1_tok], w=[oTp_tok])
                            bg_tasks.append(fin)

                elif kind == "B":
                    VpA = Vb[:, :].rearrange("p (t n) -> p t n", n=192)
                    (QT, QT_tok), (KT, KT_tok) = QK_l[qk_i % 2]
                    qk_i += 1
                    proj_fm(QT, QT_tok, True)
                    proj_fm(KT, KT_tok, True)
                    run_bg(100)
                    QZ = [acc_l[h_][0][:, :].bitcast(BF16)[:, 0:S] for h_ in range(2)]
                    QZ_tok = [acc_l[h_][1] for h_ in range(2)]
                    KZ = [KZ_l[0][0], KT]
                    KZ_tok = [KZ_l[0][1], KT_tok]
                    gsl = [slice(64, 80), slice(0, 16)]
                    if idx == 0:
                        P.op("dve", CALL("memset", QZ[0][64:128, :], 0.0), w=[QZ_tok[0]])
                        P.op("dve", CALL("memset", QZ[1][0:64, :], 0.0), w=[QZ_tok[1]])
                        P.op("dve", CALL("memset", KZ[0][64:128, :], 0.0), w=[KZ_tok[0]])
                        for g in range(NTG):
                            cols = slice(g * TG, (g + 1) * TG)
                            t1, t1_tok = t1_rot.next()
                            P.dma("sp", t1[0:16, :], ind_d[:, cols], w=[t1_tok])
                            P.op("dve", CALL("tensor_copy", out=KZ[0][64:80, cols], in_=t1[0:16, :]), r=[t1_tok], w=[KZ_tok[0]])
                    P.op("dve", CALL("tensor_copy", out=QZ[0][0:64, :], in_=QT[0:64, :]), r=[QT_tok], w=[QZ_tok[0]])
                    P.op("dve", CALL("tensor_copy", out=QZ[1][64:128, :], in_=QT[64:128, :]), r=[QT_tok], w=[QZ_tok[1]])
                    tiles = [([t // 4], slice(t * 128, (t + 1) * 128)) for t in range(NT)]
                    proj_v(tiles,
                           lambda quad: VpA[:, quad * 4:(quad + 1) * 4, :].rearrange("p t (h c) -> p t h c", c=64)[:, :, 0:3:2, :],
                           lambda bk: bk[:, :].rearrange("p (t h c) -> p t h c", t=4, h=2))
                    run_bg(100)
                    P.op("dve", CALL("tensor_reduce", out=km32[:, :], in_=KT[:, :].rearrange("p (n k) -> p n k", k=256), axis=AX.X, op=ALU.add),
                         r=[KT_tok], w=[km32_tok])
                    P.op("dve", CALL("tensor_copy", out=kmh[:, :], in_=km32[:, :]), r=[km32_tok], w=[kmh_tok])
                    P.op("dve", CALL("tensor_tensor", out=kml[:, :], in0=km32[:, :], in1=kmh[:, :], op=ALU.subtract),
                         r=[km32_tok, kmh_tok], w=[kml_tok])
                    P.op("dve", CALL("tensor_copy", out=KZ[0][0:64, :], in_=KT[0:64, :]), r=[KT_tok], w=[KZ_tok[0]])
                    P.op("dve", CALL("memset", KT[0:64, :], 0.0), w=[KT_tok])
                    P.op("dve", CALL("tensor_copy", out=KT[0:16, :], in_=KZ[0][64:80, :]), r=[KZ_tok[0]], w=[KT_tok])
                    Sb_rot = Rot(banks[0:3])
                    Ob_rot = Rot(banks[3:5])
                    gbk, gbk_tok = banks[5]
                    tbk, tbk_tok = banks[6]
                    m8_rot, sel_rot, nm_rot = Rot(m8_l), Rot(sel_l), Rot(nm_l)
                    for hh in range(2):
                        gt, gt_tok = gate_l[hh]
                        P.op("dve", CALL("memset", gt[:, :], -1e30), w=[gt_tok])
                    for (nm, nm_tok) in nm_l:
                        P.op("dve", CALL("memset", nm[:, :], 0.0), w=[nm_tok])

                    nm_of = {}

                    def emit_gate_a(G):
                        for hh in range(2):
                            hp = slice(hh * 64, (hh + 1) * 64)
                            for i in range(4):
                                qt = 4 * G + i
                                own = qt // 2
                                qsl = slice(qt * 128, (qt + 1) * 128)
                                nm, nm_tok = nm_rot.next()
                                nm_of[(G, hh, i)] = (nm, nm_tok)
                                if own >= 4:
                                    P.op("pe", CALL("matmul", gbk[:, hh * 16:hh * 16 + 16], lhsT=QT[hp, qsl], rhs=kmh[hp, :], start=True, stop=False),
                                         r=[QT_tok, kmh_tok], w=[gbk_tok])
                                    P.op("pe", CALL("matmul", gbk[:, hh * 16:hh * 16 + 16], lhsT=QT[hp, qsl], rhs=kml[hp, :], start=False, stop=True),
                                         r=[QT_tok, kml_tok], w=[gbk_tok])
                                    gt, gt_tok = gate_l[hh]
                                    P.op("dve", CALL("tensor_copy", out=gt[:, 0:own], in_=gbk[:, hh * 16:hh * 16 + own]),
                                         r=[gbk_tok], w=[gt_tok])
                                    m8, m8_tok = m8_rot.next()
                                    P.op("dve", CALL("max", out=m8[:, :], in_=gt[:, :]), r=[gt_tok], w=[m8_tok])
                                    sel, sel_tok = sel_rot.next()
                                    P.op("dve", CALL("tensor_scalar", out=sel[:, :], in0=gt[:, :], scalar1=m8[:, 2:3], scalar2=None, op0=ALU.is_ge),
                                         r=[gt_tok, m8_tok], w=[sel_tok])
                                    P.op("dve", CALL("tensor_scalar", out=nm[:, 0:own], in0=sel[:, 0:own], scalar1=-NEG, scalar2=NEG, op0=ALU.mult, op1=ALU.add),
                                         r=[sel_tok], w=[nm_tok])

                    def emit_gate_b(G):
                        for hh in range(2):
                            for i in range(4):
                                nm, nm_tok = nm_of.pop((G, hh, i))
                                P.op("pe", CALL("matmul", tbk[0:16, i * 128:(i + 1) * 128], lhsT=nm[:, :], rhs=ident, start=True, stop=True),
                                     r=[nm_tok, cb_tok], w=[tbk_tok])
                            P.op("act", CALL("activation", out=QZ[hh][gsl[hh], G * TG:(G + 1) * TG], in_=tbk[0:16, :], func=AF.Copy),
                                 r=[tbk_tok], w=[QZ_tok[hh]])

                    emit_gate_a(0)
                    emit_gate_b(0)
                    emit_gate_a(1)
                    for G in range(NTG):
                        if G >= 1:
                            emit_gate_b(G)
                            if G + 1 < NTG:
                                emit_gate_a(G + 1)
                        for hh in range(2):
                            nk = 4 * G + 4
                            ob, ob_tok = Ob_rot.next()
                            vc = slice(hh * 64, hh * 64 + 128)
                            pipe = [None, None]
                            for kt in list(range(nk)) + [None, None]:
                                cur = None
                                prev = pipe[0]
                                if kt is not None:
                                    j = kt - 4 * G
                                    c0 = max(j, 0) * 128
                                    ksl = slice(kt * 128, (kt + 1) * 128)
                                    sbk, sbk_tok = Sb_rot.next()
                                    P.op("pe", CALL("matmul", sbk[:, c0:512], lhsT=KZ[hh][:, ksl], rhs=QZ[hh][:, G * TG + c0:(G + 1) * TG],
                                                    start=True, stop=(j < 0)),
                                         r=[KZ_tok[hh], QZ_tok[hh]], w=[sbk_tok])
                                    if j >= 0:
                                        P.op("pe", CALL("matmul", sbk[:, c0:c0 + 128], lhsT=ident, rhs=maskLE, start=False, stop=True),
                                             r=[cb_tok], w=[sbk_tok])
                                    pt, pt_tok = PT_rot.next()
                                    P.op("act", CALL("activation", out=pt[:, c0:512], in_=sbk[:, c0:512], func=AF.Exp, scale=0.125),
                                         r=[sbk_tok], w=[pt_tok])
                                    cur = (kt, c0, pt, pt_tok)
                                if prev is not None:
                                    kt2, c02, pt, pt_tok = prev
                                    P.op("pe", CALL("matmul", ob[:, c02:512], lhsT=VpA[:, kt2, vc], rhs=pt[:, c02:512],
                                                    start=(kt2 == 0), stop=(kt2 == nk - 1)),
                                         r=[pt_tok, Vb_tok], w=[ob_tok])
                                pipe = [pipe[1], cur]
                            cols = slice(G * TG, (G + 1) * TG)
                            nump = slice(hh * 64, (hh + 1) * 64)
                            denp = slice((1 - hh) * 64, (2 - hh) * 64)
                            t1, t1_tok = t1_rot.next()
                            P.op("dve", CALL("reciprocal", out=t1[nump, :], in_=ob[denp, :]), r=[ob_tok], w=[t1_tok])
                            P.op("dve", CALL("tensor_tensor", out=oTp[nump, cols], in0=ob[nump, :], in1=t1[nump, :], op=ALU.mult),
                                 r=[ob_tok, t1_tok], w=[oTp_tok])

                else:
                    VpC = Vb[:, 0:NT * 128].rearrange("p (t n) -> p t n", n=128)
                    (QT, QT_tok), (KT, KT_tok) = QK_l[qk_i % 2]
                    qk_i += 1
                    proj_fm(QT, QT_tok, False)
                    proj_fm(KT, KT_tok, False)
                    run_bg(100)
                    QZ = [acc_l[h_][0][:, :].bitcast(BF16)[:, 0:S] for h_ in range(2)]
                    QZ_tok = [acc_l[h_][1] for h_ in range(2)]
                    if idx == 0:
                        P.op("dve", CALL("memset", QZ[0][64:80, :], 0.0), w=[QZ_tok[0]])
                        P.op("dve", CALL("memset", QZ[1][0:16, :], 0.0), w=[QZ_tok[1]])
                    P.op("dve", CALL("tensor_copy", out=QZ[0][0:64, :], in_=QT[0:64, :]), r=[QT_tok], w=[QZ_tok[0]])
                    P.op("dve", CALL("tensor_copy", out=QZ[1][64:128, :], in_=QT[64:128, :]), r=[QT_tok], w=[QZ_tok[1]])
                    tiles = [([t // 4], slice(t * 128, (t + 1) * 128)) for t in range(NT)]
                    proj_v(tiles,
                           lambda quad: VpC[:, quad * 4:(quad + 1) * 4, :],
                           lambda bk: bk[:, :].rearrange("p (t n) -> p t n", t=4))
                    Z_rot = Rot(banks[0:2])
                    C_rot = Rot(banks[2:4])
                    O_rot = Rot(banks[4:6])
                    E_rot, Lp_rot, Aw_rot, Ec_rot = Rot(E_l), Rot(Lp_l), Rot(Aw_l), Rot(Ec_l)
                    jobs = [(qt, hh) for qt in range(NT) for hh in range(2)]
                    s1 = None
                    s2 = None
                    obank = None
                    for job in jobs + [None, None]:
                        c1 = None
                        if job is not None:
                            qt, hh = job
                            hp = slice(hh * 64, (hh + 1) * 64)
                            qsl = slice(qt * 128, (qt + 1) * 128)
                            kts = [kt for kt in range(max(0, qt - SB_WIN + 1), qt + 1)]
                            zb, zb_tok = Z_rot.next()
                            for j, kt in enumerate(kts):
                                ksl = slice(kt * 128, (kt + 1) * 128)
                                P.op("pe", CALL("matmul", zb[:, j * 128:(j + 1) * 128], lhsT=KT[:, ksl], rhs=QZ[hh][:, qsl], start=True, stop=True),
                                     r=[KT_tok, QZ_tok[hh]], w=[zb_tok])
                            ncol = len(kts) * 128
                            E, E_tok = E_rot.next()
                            Lp, Lp_tok = Lp_rot.next()
                            P.op("act", CALL("activation", out=E[:, 0:ncol], in_=zb[:, 0:ncol], func=AF.Exp, scale=0.125),
                                 r=[zb_tok], w=[E_tok])
                            P.op("dve", CALL("tensor_tensor", out=E[:, ncol - 128:ncol], in0=E[:, ncol - 128:ncol], in1=m01S, op=ALU.mult),
                                 r=[E_tok, cb_tok], w=[E_tok])
                            P.op("act", CALL("activation", out=Lp[:, 0:ncol], in_=E[:, 0:ncol], func=AF.Ln, bias=1.0, scale=1.0),
                                 r=[E_tok], w=[Lp_tok])
                            c1 = (qt, hh, kts, Lp, Lp_tok, E, E_tok)
                        c2 = None
                        if s1 is not None:
                            qt, hh, kts, Lp, Lp_tok, E, E_tok = s1
                            hp = slice(hh * 64, (hh + 1) * 64)
                            qsl = slice(qt * 128, (qt + 1) * 128)
                            cbk, cbk_tok = C_rot.next()
                            nk = len(kts)
                            for j, kt in enumerate(kts):
                                ksl = slice(kt * 128, (kt + 1) * 128)
                                reg = slice(j * 128, (j + 1) * 128)
                                P.op("pe", CALL("matmul", cbk[:, reg], lhsT=uincl, rhs=Lp[:, j * 128:(j + 1) * 128], start=True, stop=(j == nk - 1)),
                                     r=[Lp_tok, cb_tok], w=[cbk_tok])
                                for j2 in range(j + 1, nk):
                                    P.op("pe", CALL("matmul", cbk[:, reg], lhsT=negones, rhs=Lp[:, j2 * 128:(j2 + 1) * 128], start=False, stop=(j2 == nk - 1)),
                                         r=[Lp_tok, cb_tok], w=[cbk_tok])
                            Aw, Aw_tok = Aw_rot.next()
                            Ec, Ec_tok = Ec_rot.next()
                            ncol = nk * 128
                            P.op("act", CALL("activation", out=Ec[:, 0:ncol], in_=cbk[:, 0:ncol], func=AF.Exp, scale=0.125),
                                 r=[cbk_tok], w=[Ec_tok])
                            P.op("dve", CALL("tensor_tensor", out=Aw[:, 0:ncol], in0=E[:, 0:ncol], in1=Ec[:, 0:ncol], op=ALU.mult),
                                 r=[E_tok, Ec_tok], w=[Aw_tok])
                            c2 = (qt, hh, kts, Aw, Aw_tok)
                        if s2 is not None:
                            qt, hh, kts, Aw, Aw_tok = s2
                            hp = slice(hh * 64, (hh + 1) * 64)
                            q4 = qt % 4
                            if q4 == 0 and hh == 0:
                                obank = O_rot.next()
                            ob, ob_tok = obank
                            nk = len(kts)
                            for j, kt in enumerate(kts):
                                P.op("pe", CALL("matmul", ob[hp, q4 * 128:(q4 + 1) * 128], lhsT=VpC[:, kt, hp], rhs=Aw[:, j * 128:(j + 1) * 128],
                                                                                                         start=(j == 0), stop=(j == nk - 1)),
                                     r=[Aw_tok, Vb_tok], w=[ob_tok])
                            if q4 == 3 and hh == 1:
                                cols = slice((qt - 3) * 128, (qt + 1) * 128)
                                P.op("dve", CALL("tensor_copy", out=oTp[:, cols], in_=ob[:, :]),
                                     r=[ob_tok], w=[oTp_tok])
                        s2 = c2
                        s1 = c1

                chk(20)
                if ui == 0 and l == 0:
                    conv_layer(0, 1, after=[P.ops["pe"][-1]])
                if ui == 1 and l + 1 < L:
                    conv_layer(l + 1, 0, after=[P.ops["pe"][-1]])
                    conv_layer(l + 1, 1)
                def out_dma(ui=ui):
                    P.dma("sp", oT[ui * 128:(ui + 1) * 128, :], oTp[:, :], r=[oTp_tok], w=[oT_tok[ui]])
                bg_tasks.append(out_dma)
            run_bg(100)

            P.barrier()
            ph.close()
            if stop_at <= 9:
                break
            ph = contextlib.ExitStack()
            cur_st[0] = ph
            xg_l = sbn("xg", [128, NCH, TG], F32, 1)
            og_l = sbn("og", [128, NCH, TG], BF16, 1)
            sg_l = sbn("sg", [128, TG], F32, 3)
            mt_l = sbn("mt", [128, TG], F32, 2)
            mg, mg_tok = sb("mg", [128, NCH, TG], BF16)
            h2 = mg
            rl_l = sbn("rl", [128, TG], F32, 2)
            uT, uT_tok = sb("uT", [128, 32, TG], BF16)
            wg_l = sbn("wg", [128, NCH, 3, 128], BF16, 2)
            wo3_l = sbn("wo3", [128, NCH, 128], BF16, 2)
            wo4_l = sbn("wo4", [128, NCH, 128], BF16, 4)
            wup_l = sbn("wupb", [128, NCH, 256], BF16, 3)
            wdn_l = sbn("wdnb", [128, 2, 1024], BF16, 3)
            fo_l = rl_l
            wg_stream = WStream(P, "sp", wg_l,
                                [(lambda t, cb_=cb_: (t[:, :, :, :], wgate_b[l, cb_].rearrange("p (c b n) -> p c b n", b=3, n=128), [wtok[("gate", l, cb_)]]))
                                 for g in range(NTG) for cb_ in range(8)], ahead=1)
            wo3_stream = WStream(P, "sp", wo3_l,
                                 [(lambda t, cb_=cb_: (t[:, :, :], wout_b[l, cb_].rearrange("p (c n) -> p c n", n=128), [wtok[("out", l, cb_)]]))
                                  for g in range(NTG) for cb_ in range(8)], ahead=1)
            wo4_stream = WStream(P, "sp", wo4_l,
                                 [(lambda t, cb_=cb_: (t[:, :, :], wo_b[l, cb_].rearrange("p (c n) -> p c n", n=128), [wtok[("o", l, cb_)]]))
                                  for g in range(NTG) for cb_ in range(8)], ahead=3)
            wup_stream = WStream(P, "sp", wup_l,
                                 [(lambda t, j4=j4: (t[:, :, :], wup_b[l, j4].rearrange("p (c n) -> p c n", n=256), [wtok[("up", l, j4)]]))
                                  for g in range(NTG) for j4 in range(16)], ahead=2)
            wdn_stream = WStream(P, "sp", wdn_l,
                                 [(lambda t, j4=j4: (t[:, :, :], wdn_b[l, j4].rearrange("p (j n) -> p j n", n=1024), [wtok[("dn", l, j4)]]))
                                  for g in range(NTG) for j4 in range(16)], ahead=2)
            allb = Rot(banks[0:7])
            statb = banks[7]
            sg_rot, mt_rot, rl_rot = Rot(sg_l), Rot(mt_l), Rot(rl_l)
            last_layer = (l == L - 1)
            h2_toks = [P.buf("h2c%d" % c) for c in range(NCH)]
            tail = [None]
            for g in range(NTG):
                cols = slice(g * TG, (g + 1) * TG)
                xg, xg_tok = xg_l[0]
                og, og_tok = og_l[0]
                if g == 0:
                    P.dma("sp", og[:, :, :], oT.rearrange("(c p) t -> p c t", p=128)[:, :, cols], r=oT_tok, w=[og_tok])
                xload = tail[0] is not None
                if not xload:
                    P.dma("sp", xg[:, :, :], x_src_ap(l, g), r=([xs_tok[g]] if l > 0 else []), w=[xg_tok])
                for cb_ in range(8):
                    if cb_ == 1 and tail[0] is not None:
                        tail[0]()
                        tail[0] = None
                    if cb_ == 5 and xload:
                        P.dma("sp", xg[:, :, :], x_src_ap(l, g), r=([xs_tok[g]] if l > 0 else []), w=[xg_tok])
                    wg_, wg_tok = wg_stream.get(g * 8 + cb_)
                    w3, w3_tok = wo3_stream.get(g * 8 + cb_)
                    mt, mt_tok = mt_rot.next()
                    for br, (c0, c1) in enumerate(((0, 2), (2, 5), (5, 8))):
                        gb, gb_tok = allb.next()
                        for c in range(NCH):
                            P.op("pe", CALL("matmul", gb[:, :], lhsT=wg_[:, c, br, :], rhs=hT[:, c, cols], start=(c == 0), stop=(c == NCH - 1)),
                                 r=[wg_tok, hT_tok[g]], w=[gb_tok])
                        sg, sg_tok = sg_rot.next()
                        P.op("act", CALL("activation", out=sg[:, :], in_=gb[:, :], func=AF.Sigmoid),
                             r=[gb_tok], w=[sg_tok])
                        yb, yb_tok = allb.next()
                        for c in range(c0, c1):
                            P.op("pe", CALL("matmul", yb[:, :], lhsT=w3[:, c, :], rhs=og[:, c, :], start=(c == c0), stop=(c == c1 - 1)),
                                 r=[w3_tok, og_tok], w=[yb_tok])
                        if br == 0:
                            P.op("dve", CALL("tensor_tensor", out=mt[:, :], in0=yb[:, :], in1=sg[:, :], op=ALU.mult),
                                 r=[yb_tok, sg_tok], w=[mt_tok])
                        else:
                            t2, t2_tok = t2_rot.next()
                            P.op("dve", CALL("tensor_tensor", out=t2[:, :], in0=yb[:, :], in1=sg[:, :], op=ALU.mult),
                                 r=[yb_tok, sg_tok], w=[t2_tok])
                            if br == 1:
                                P.op("dve", CALL("tensor_tensor", out=mt[:, :], in0=mt[:, :], in1=t2[:, :], op=ALU.add),
                                     r=[mt_tok, t2_tok], w=[mt_tok])
                            else:
                                P.op("dve", CALL("tensor_tensor", out=mg[:, cb_, :], in0=mt[:, :], in1=t2[:, :], op=ALU.add),
                                     r=[mt_tok, t2_tok], w=[h2_toks[cb_]])
                for cb_ in range(8):
                    w4, w4_tok = wo4_stream.get(g * 8 + cb_)
                    rb, rb_tok = allb.next()
                    for c in range(NCH):
                        P.op("pe", CALL("matmul", rb[:, :], lhsT=w4[:, c, :], rhs=mg[:, c, :], start=(c == 0), stop=(c == NCH - 1)),
                             r=[w4_tok, h2_toks[c]], w=[rb_tok])
                    P.op("dve", CALL("tensor_tensor", out=xg[:, cb_, :], in0=rb[:, :], in1=xg[:, cb_, :], op=ALU.add),
                         r=[rb_tok, xg_tok], w=[xg_tok])
                    if cb_ >= 2:
                        norm_stat(xg, xg_tok, statb, cb_ - 2)
                    if cb_ == 0 and g + 1 < NTG:
                        ncols = slice((g + 1) * TG, (g + 2) * TG)
                        P.dma("sp", og[:, :, :], oT.rearrange("(c p) t -> p c t", p=128)[:, :, ncols], r=oT_tok, w=[og_tok])
                norm_stat(xg, xg_tok, statb, NCH - 2)
                norm_stat(xg, xg_tok, statb, NCH - 1)
                norm_finish(xg, xg_tok, statb, lambda c: gain_ap(l, 1, c), lambda c: h2[:, c, :], None, per_chunk=lambda c: h2_toks[c])
                for j4 in range(16):
                    wu, wu_tok = wup_stream.get(g * 16 + j4)
                    for jj in range(2):
                        jb = j4 * 2 + jj
                        ub, ub_tok = allb.next()
                        for c in range(NCH):
                            P.op("pe", CALL("matmul", ub[:, :], lhsT=wu[:, c, jj * 128:(jj + 1) * 128], rhs=h2[:, c, :], start=(c == 0), stop=(c == NCH - 1)),
                                 r=[wu_tok, h2_toks[c]], w=[ub_tok])
                        rl, rl_tok = rl_rot.next()
                        P.op("act", CALL("activation", out=rl[:, :], in_=ub[:, :], func=AF.Relu),
                             r=[ub_tok], w=[rl_tok])
                        P.op("dve", CALL("tensor_tensor", out=uT[:, jb, :], in0=rl[:, :], in1=rl[:, :], op=ALU.mult),
                             r=[rl_tok], w=[uT_tok])
                for j4 in range(16):
                    wd, wd_tok = wdn_stream.get(g * 16 + j4)
                    for jj in range(2):
                        jc = j4 * 2 + jj
                        for cb_ in range(8):
                            db, db_tok = banks[cb_]
                            P.op("pe", CALL("matmul", db[:, :], lhsT=wd[:, jj, cb_ * 128:(cb_ + 1) * 128], rhs=uT[:, jc, :], start=(jc == 0), stop=(jc == 31)),
                                 r=[wd_tok, uT_tok], w=[db_tok])
                for cb_ in range(8):
                    db, db_tok = banks[cb_]
                    P.op("dve", CALL("tensor_tensor", out=xg[:, cb_, :], in0=db[:, :], in1=xg[:, cb_, :], op=ALU.add),
                         r=[db_tok, xg_tok], w=[xg_tok])
                if last_layer and final_norm:
                    def tail_fn(xg=xg, xg_tok=xg_tok, cols=cols):
                        fo_rot = Rot(fo_l)
                        fo_cur = {}

                        def fo_out(c):
                            if c not in fo_cur:
                                fo_cur[c] = fo_rot.next()
                            return fo_cur[c][0][:, :]
                        for c in range(NCH):
                            norm_stat(xg, xg_tok, statb, c)
                        norm_finish(xg, xg_tok, statb, gain_final, fo_out, None, per_chunk=lambda c: fo_cur[c][1],
                                    after_chunk=lambda c: out_dmas.append(
                                        P.dma("sp", outT[c * 128:(c + 1) * 128, cols], fo_cur[c][0][:, :], r=[fo_cur[c][1]])))
                    tail[0] = tail_fn
                elif last_layer:
                    out_dmas.append(P.dma("sp", outT.rearrange("(c p) t -> p c t", p=128)[:, :, cols], xg[:, :, :], r=[xg_tok]))
                else:
                    P.dma("sp", xs.rearrange("(c p) t -> p c t", p=128)[:, :, cols], xg[:, :, :], r=[xg_tok], w=[xs_tok[g]])

                    def tail_fn(xg=xg, xg_tok=xg_tok, cols=cols, g=g):
                        for c in range(NCH):
                            norm_stat(xg, xg_tok, statb, c)
                        norm_finish(xg, xg_tok, statb, lambda c: gain_ap(l + 1, 0, c), lambda c: hT[:, c, cols], [hT_tok[g]])
                    tail[0] = tail_fn
            if tail[0] is not None:
                tail[0]()
                tail[0] = None
            P.barrier()
            ph.close()
            cur_st[0] = st

        STOPPED[0] = False
        if not out_dmas:
            cur_st[0] = st
            out_dmas.append(P.dma("sp", outT[0:128, 0:200], gains[:, :], r=[gains_tok]))
        P.emit(final_wait_ops=out_dmas)
    return nc


def _consts():
    half = 8
    inv_freq = np.exp(-np.log(np.float32(500000.0)) * np.arange(0, 16, 2, dtype=np.float32) / np.float32(16)).astype(np.float32)
    pos = np.arange(S, dtype=np.float32)
    ang = (pos[:, None] * inv_freq[None, :]).astype(np.float32)
    cos = np.cos(ang).astype(np.float32).T
    sin = np.sin(ang).astype(np.float32).T
    cf = np.zeros((128, 2, S), np.float32)
    for hb in (0, 64):
        cf[hb:hb + 64, 0, :] = 1.0
        cf[hb:hb + 8, 0, :] = cos
        cf[hb + 8:hb + 16, 0, :] = cos
        cf[hb:hb + 8, 1, :] = -sin
        cf[hb + 8:hb + 16, 1, :] = sin
    cb = np.zeros((128, 1408), np.float32)
    cb[:, 0:128] = np.eye(128, dtype=np.float32)
    pm = np.zeros((128, 128), np.float32)
    for hb in (0, 64):
        for m in range(8):
            pm[hb + m + 8, hb + m] = 1.0
            pm[hb + m, hb + m + 8] = 1.0
    cb[:, 128:256] = pm
    k = np.arange(128)[:, None]
    q = np.arange(128)[None, :]
    cb[:, 256:384] = np.where(k >= q, 0.0, NEG)
    cb[:, 384:512] = np.where(k <= q, 0.0, NEG)
    cb[:, 512:640] = np.where(k < q, 0.0, NEG)
    cb[:, 640:768] = np.where(k >= q, -8.0, 0.0)
    cb[:, 768:896] = -8.0
    cb[:, 896:1024] = np.where(k >= q, 1.0, 0.0)
    cb[:, 1024:1152] = np.where(k <= q, 1.0, 0.0)
    cb[:, 1152:1280] = np.where(k < q, 1.0, 0.0)
    cb[:, 1280:1408] = 1.0 / D
    return cf, cb


def _layout_weights(w_in, w_out_a, w_out_b, w_out_c, w_o, w_up, w_down, L):
    w_in = np.asarray(w_in)[:L]
    wqkv = np.ascontiguousarray(w_in[:, :, :4608].reshape(L, 8, 128, 36, 128).transpose(0, 3, 2, 1, 4)).reshape(L, 36, 128, 8 * 128)
    wgate = np.ascontiguousarray(w_in[:, :, 4608:].reshape(L, 8, 128, 3, 8, 128).transpose(0, 4, 2, 1, 3, 5)).reshape(L, 8, 128, 8 * 3 * 128)
    wcat = np.concatenate([np.asarray(w_out_a)[:L], np.asarray(w_out_b)[:L], np.asarray(w_out_c)[:L]], axis=1)
    wout = np.ascontiguousarray(wcat.reshape(L, 8, 128, 8, 128).transpose(0, 3, 2, 1, 4)).reshape(L, 8, 128, 8 * 128)
    wo = np.ascontiguousarray(np.asarray(w_o)[:L].reshape(L, 8, 128, 8, 128).transpose(0, 3, 2, 1, 4)).reshape(L, 8, 128, 8 * 128)
    wup = np.ascontiguousarray(np.asarray(w_up)[:L].reshape(L, 8, 128, 16, 256).transpose(0, 3, 2, 1, 4)).reshape(L, 16, 128, 8 * 256)
    wdn = np.ascontiguousarray(np.asarray(w_down)[:L].reshape(L, 16, 2, 128, 1024).transpose(0, 1, 3, 2, 4)).reshape(L, 16, 128, 2 * 1024)
    return wqkv, wgate, wout, wo, wup, wdn


def _gains(norm_mix, norm_mlp, norm_final):
    g = np.zeros((128, DEPTH * 16 + 8 + 128), np.float32)
    g[:, DEPTH * 16 + 8:] = 1.0 / D
    nm = np.asarray(norm_mix).reshape(DEPTH, 8, 128)
    nl = np.asarray(norm_mlp).reshape(DEPTH, 8, 128)
    for l in range(DEPTH):
        g[:, l * 16:l * 16 + 8] = nm[l].T
        g[:, l * 16 + 8:l * 16 + 16] = nl[l].T
    g[:, DEPTH * 16:DEPTH * 16 + 8] = np.asarray(norm_final).reshape(8, 128).T
    return g


def make_in_maps(x, norm_mix, w_in, w_out_a, w_out_b, w_out_c, w_o, norm_mlp, w_up, w_down, norm_final, L=DEPTH, cores=8):
    cf, cb = _consts()
    wqkv, wgate, wout, wo, wup, wdn = _layout_weights(w_in, w_out_a, w_out_b, w_out_c, w_o, w_up, w_down, L)
    g = _gains(norm_mix, norm_mlp, norm_final)
    x = np.asarray(x)
    ind = (np.arange(S)[None, :] // 256 == np.arange(16)[:, None]).astype(np.float32)
    maps = []
    for b in range(cores):
        maps.append({"xT": np.ascontiguousarray(x[b].T), "wqkv": wqkv, "wgate": wgate, "wout": wout, "wo": wo,
                     "wup": wup, "wdn": wdn, "gains": g, "cf": cf, "cb": cb, "ind": ind})
    return maps


def kernel(x, norm_mix, w_in, w_out_a, w_out_b, w_out_c, w_o, norm_mlp, w_up, w_down, norm_final):
    nc = build(DEPTH)
    maps = make_in_maps(x, norm_mix, w_in, w_out_a, w_out_b, w_out_c, w_o, norm_mlp, w_up, w_down, norm_final)
    res = run_bass_kernel_spmd(nc, maps, core_ids=list(range(8)))
    out = np.stack([np.asarray(r["outT"]).T for r in res.results], axis=0)
    return np.ascontiguousarray(out.astype(np.float32))
```

```python
import contextlib
import numpy as np
import concourse.bass as bass
import concourse.mybir as mybir
from concourse.bass_utils import run_bass_kernel_spmd

F32 = mybir.dt.float32
BF16 = mybir.dt.bfloat16
AF = mybir.ActivationFunctionType
ALU = mybir.AluOpType
AX = mybir.AxisListType

D = 1024
S = 4096
DEPTH = 4
NCH = 8
TG = 512
NTG = S // TG
NT = S // 128
DFF = 4096
NEG = -30000.0
SB_WIN = 3
EPS = 1e-6
DIL = (1, 4, 16)

ENGS = ("pe", "act", "dve", "pool", "sp")
SAME_ENGINE_SYNC = {"pe": False, "act": True, "dve": True, "pool": True, "sp": False}
NSLOT = 16
SEM_WRAP = 30000


class Buf:
    __slots__ = ("name", "last_w", "readers", "excl")

    def __init__(self, name, excl=False):
        self.name = name
        self.last_w = None
        self.readers = []
        self.excl = excl


class Op:
    __slots__ = ("eng", "fn", "deps", "is_dma", "signal", "needed")

    def __init__(self, eng, fn, is_dma):
        self.eng = eng
        self.fn = fn
        self.deps = []
        self.is_dma = is_dma
        self.signal = None
        self.needed = False


class Prog:
    def __init__(self, nc):
        self.nc = nc
        self.ops = {e: [] for e in ENGS}
        self.dmas = {e: [] for e in ENGS}

    def buf(self, name="b"):
        return Buf(name)

    def op(self, eng, fn, r=(), w=(), dma=False, after=()):
        o = Op(eng, fn, dma)
        if STOPPED[0]:
            return o
        deps = list(after)
        if any(b.excl for b in r):
            w = list(w) + [b for b in r if b.excl and b not in w]
            r = [b for b in r if not b.excl]
        for b in r:
            if b.last_w is not None:
                deps.append(b.last_w)
        for b in w:
            if b.last_w is not None:
                deps.append(b.last_w)
            deps.extend(b.readers)
        for b in r:
            b.readers.append(o)
        for b in w:
            b.last_w = o
            b.readers = []
        seen = set()
        for d in deps:
            if d is o or id(d) in seen:
                continue
            if d.eng == eng and not d.is_dma and not SAME_ENGINE_SYNC[eng]:
                continue
            seen.add(id(d))
            o.deps.append(d)
            d.needed = True
        self.ops[eng].append(o)
        if dma:
            self.dmas[eng].append(o)
        return o

    def dma(self, eng, out, in_, r=(), w=()):
        return self.op(eng, lambda e: e.dma_start(out=out, in_=in_), r=r, w=w, dma=True)

    def barrier(self):
        engs = [e for e in ENGS if e != "pool"]
        lasts = []
        for e in engs:
            if self.ops[e]:
                lasts.append(self.ops[e][-1])
            lasts.extend(self.dmas[e][-NSLOT:])
        for e in engs:
            self.op(e, lambda eng: eng.nop(), after=lasts)

    def emit(self, final_wait_ops=()):
        nc = self.nc
        with contextlib.ExitStack() as st:
            sem_cache = {}

            def get_sem(key):
                if key not in sem_cache:
                    sem_cache[key] = st.enter_context(nc.semaphore("s_%s" % "_".join(map(str, key))))
                return sem_cache[key]

            for f in final_wait_ops:
                f.needed = True
            for e in ENGS:
                cnt = 0
                gen = 0
                dcount = 0
                slot_prev = {}
                for o in self.ops[e]:
                    if o.is_dma:
                        slot = dcount % NSLOT
                        use = dcount // NSLOT + 1
                        dcount += 1
                        sem = get_sem((e, "d", slot))
                        o.signal = (sem, 16 * use, 16)
                        prev = slot_prev.get(slot)
                        if prev is not None:
                            o.deps.append(prev)
                        slot_prev[slot] = o
                    elif o.needed:
                        cnt += 1
                        if cnt > SEM_WRAP:
                            gen += 1
                            cnt = 1
                        sem = get_sem((e, "c", gen))
                        o.signal = (sem, cnt, 1)
            engmap = {"pe": "tensor", "act": "scalar", "dve": "vector", "pool": "gpsimd", "sp": "sync"}
            with nc.Block() as block:
                for e in ENGS:
                    ops = self.ops[e]
                    extra = list(final_wait_ops) if e == "sp" else []

                    def body(eng, ops=ops, extra=extra):
                        waited = {}

                        def do_wait(d):
                            sem, val, _ = d.signal
                            k = id(sem)
                            if waited.get(k, 0) >= val:
                                return
                            waited[k] = val
                            eng.wait_ge(sem, val)

                        for o in ops:
                            for d in o.deps:
                                do_wait(d)
                            if isinstance(o.fn, tuple):
                                ins = getattr(eng, o.fn[0])(*o.fn[1], **o.fn[2])
                            else:
                                ins = o.fn(eng)
                            if o.signal is not None:
                                ins.then_inc(o.signal[0], o.signal[2])
                        for f in extra:
                            do_wait(f)

                    getattr(block, engmap[e])(body)


def CALL(name, *a, **k):
    return (name, a, k)


import os
SUB = int(os.environ.get("MK_SUB", "0"))
STOPPED = [False]


def chk(n):
    if SUB == n:
        STOPPED[0] = True


class Rot:
    def __init__(self, items):
        self.items = list(items)
        self.i = 0

    def next(self):
        it = self.items[self.i % len(self.items)]
        self.i += 1
        return it


class WStream:
    def __init__(self, P, eng, bufs, loads, ahead):
        self.P, self.eng, self.bufs, self.loads = P, eng, bufs, loads
        self.ahead = min(ahead, len(bufs) - 1)
        self.issued = 0

    def _issue(self, k):
        t, tok = self.bufs[k % len(self.bufs)]
        out_ap, in_ap, rt = self.loads[k](t)
        self.P.dma(self.eng, out_ap, in_ap, r=rt, w=[tok])

    def get(self, i):
        lim = min(i + self.ahead, len(self.loads) - 1)
        while self.issued <= lim:
            self._issue(self.issued)
            self.issued += 1
        return self.bufs[i % len(self.bufs)]


def build(n_layers=DEPTH, debug=False, final_norm=True, stop_at=99):
    nc = bass.Bass("TRN2", target_bir_lowering=False)
    L = n_layers
    xT_in = nc.dram_tensor("xT", [D, S], F32, kind="ExternalInput").ap()
    wqkv = nc.dram_tensor("wqkv", [L, 36, 128, NCH * 128], F32, kind="ExternalInput").ap()
    wgate = nc.dram_tensor("wgate", [L, 8, 128, NCH * 3 * 128], F32, kind="ExternalInput").ap()
    wout = nc.dram_tensor("wout", [L, 8, 128, NCH * 128], F32, kind="ExternalInput").ap()
    wo = nc.dram_tensor("wo", [L, 8, 128, NCH * 128], F32, kind="ExternalInput").ap()
    wup = nc.dram_tensor("wup", [L, 16, 128, NCH * 256], F32, kind="ExternalInput").ap()
    wdn = nc.dram_tensor("wdn", [L, 16, 128, 2 * 1024], F32, kind="ExternalInput").ap()
    gains_d = nc.dram_tensor("gains", [128, DEPTH * 16 + 8 + 128], F32, kind="ExternalInput").ap()
    cf_d = nc.dram_tensor("cf", [128, 2, S], F32, kind="ExternalInput").ap()
    cb_d = nc.dram_tensor("cb", [128, 1408], F32, kind="ExternalInput").ap()
    ind_d = nc.dram_tensor("ind", [16, S], F32, kind="ExternalInput").ap()
    outT = nc.dram_tensor("outT", [D, S], F32, kind="ExternalOutput").ap()
    xs = nc.dram_tensor("xs", [D, S], F32).ap()
    wqkv_b = nc.dram_tensor("wqkv_b", [L, 36, 128, NCH * 128], BF16).ap()
    wgate_b = nc.dram_tensor("wgate_b", [L, 8, 128, NCH * 3 * 128], BF16).ap()
    wout_b = nc.dram_tensor("wout_b", [L, 8, 128, NCH * 128], BF16).ap()
    wo_b = nc.dram_tensor("wo_b", [L, 8, 128, NCH * 128], BF16).ap()
    wup_b = nc.dram_tensor("wup_b", [L, 16, 128, NCH * 256], BF16).ap()
    wdn_b = nc.dram_tensor("wdn_b", [L, 16, 128, 2 * 1024], BF16).ap()
    if debug:
        oT = nc.dram_tensor("oT", [D, S], BF16, kind="ExternalOutput").ap()
    else:
        oT = nc.dram_tensor("oT", [D, S], BF16).ap()

    P = Prog(nc)
    st = contextlib.ExitStack()
    with st:
        cur_st = [st]
        uid = [0]

        def sb(name, shape, dt):
            uid[0] += 1
            return cur_st[0].enter_context(nc.sbuf_tensor("%s_%d" % (name, uid[0]), shape, dt)), P.buf(name)

        def sbn(name, shape, dt, n):
            return [sb("%s%d" % (name, i), shape, dt) for i in range(n)]

        banks = []
        for i in range(8):
            banks.append((st.enter_context(nc.psum_tensor("pb%d" % i, [128, 512], F32)), Buf("pb%d" % i, excl=True)))

        hT, _ = sb("hT", [128, NCH, S], BF16)
        hT_tok = [P.buf("hT%d" % g) for g in range(NTG)]
        cbt, cb_tok = sb("cbt", [128, 1408], BF16)
        ident = cbt[:, 0:128]
        perm = cbt[:, 128:256]
        maskA = cbt[:, 256:512]
        maskLE = cbt[:, 384:512]
        maskS = cbt[:, 512:640]
        uincl = cbt[:, 640:768]
        negones = cbt[:, 768:896]
        m01A = cbt[:, 896:1152]
        m01S = cbt[:, 1152:1280]
        onesb = cbt[:, 1280:1408]
        gains, gains_tok = sb("gains", [128, DEPTH * 16 + 8 + 128], F32)
        cf_tok = gains_tok
        ones32 = gains[:, DEPTH * 16 + 8:DEPTH * 16 + 8 + 128]
        sq_l = sbn("sq", [128, TG], BF16, 3)
        rstd_l = sbn("rstd", [128, TG], F32, 2)
        t2_l = sbn("t2", [128, TG], F32, 2)

        P.dma("sp", gains[:, :], gains_d[:, :], w=[gains_tok])
        P.dma("pool", cbt[:, :], cb_d[:, :], w=[cb_tok])

        units = []
        for ap_ in range(2):
            units.append(("A", ap_))
        for bp in range(3):
            units.append(("B", bp))
        for cp in range(3):
            units.append(("C", cp))
        blk_order = []
        for kind, idx in units:
            if kind == "A":
                for gi in range(3):
                    b = 2 * gi + idx
                    blk_order += [b, 12 + b, 24 + b]
            elif kind == "B":
                blk_order += [6 + idx, 18 + idx, 30 + idx]
            else:
                blk_order += [9 + idx, 21 + idx, 33 + idx]
        wtok = {}

        def conv_layer(l, part, after=()):
            aft = [list(after)]

            def cv(key, dst, src):
                wtok[key] = P.buf("w%s" % (key,))
                o_ = P.op("pool", lambda e, dst=dst, src=src: e.dma_start(out=dst, in_=src), w=[wtok[key]], dma=True, after=aft[0])
                aft[0] = [o_] if (l > 0 or part == 1) else []
            if part == 0:
                for b in blk_order:
                    cv(("qkv", l, b), wqkv_b[l, b], wqkv[l, b])
                return
            for cb_ in range(8):
                cv(("gate", l, cb_), wgate_b[l, cb_], wgate[l, cb_])
                cv(("out", l, cb_), wout_b[l, cb_], wout[l, cb_])
            for cb_ in range(8):
                cv(("o", l, cb_), wo_b[l, cb_], wo[l, cb_])
            for j in range(16):
                cv(("up", l, j), wup_b[l, j], wup[l, j])
            for j in range(16):
                cv(("dn", l, j), wdn_b[l, j], wdn[l, j])

        conv_layer(0, 0)
        xs_tok = [P.buf("xs%d" % g) for g in range(NTG)]
        oT_tok = [P.buf("oT%d" % u) for u in range(8)]
        out_dmas = []

        def gain_ap(l, which, c):
            col = l * 16 + which * 8 + c
            return gains[:, col:col + 1]

        def gain_final(c):
            col = DEPTH * 16 + c
            return gains[:, col:col + 1]

        sq_rot = Rot(sq_l)
        rstd_rot = Rot(rstd_l)

        def norm_stat(xg, xg_tok, bank, c):
            bk, bk_tok = bank
            sq, sq_tok = sq_rot.next()
            P.op("act", CALL("activation", out=sq[:, :], in_=xg[:, c, :], func=AF.Square), r=[xg_tok], w=[sq_tok])
            P.op("pe", CALL("matmul", bk[:, :], lhsT=onesb, rhs=sq[:, :], start=(c == 0), stop=(c == NCH - 1)),
                 r=[sq_tok, cb_tok], w=[bk_tok])

        def norm_finish(xg, xg_tok, bank, gain_fn, out_fn, out_toks, per_chunk=None, after_chunk=None):
            bk, bk_tok = bank
            rstd, rstd_tok = rstd_rot.next()
            P.op("act", CALL("activation", out=rstd[:, :], in_=bk[:, :], func=AF.Ln, bias=EPS, scale=1.0),
                 r=[bk_tok], w=[rstd_tok])
            P.op("act", CALL("activation", out=rstd[:, :], in_=rstd[:, :], func=AF.Exp, scale=-0.5),
                 r=[rstd_tok], w=[rstd_tok])
            for c in range(NCH):
                oap = out_fn(c)
                wt = list(out_toks) if per_chunk is None else [per_chunk(c)]
                P.op("dve", CALL("scalar_tensor_tensor", out=oap, in0=xg[:, c, :], scalar=gain_fn(c), in1=rstd[:, :],
                                 op0=ALU.mult, op1=ALU.mult),
                     r=[xg_tok, rstd_tok, gains_tok], w=wt)
                if after_chunk is not None:
                    after_chunk(c)

        def norm_group(xg, xg_tok, bank, gain_fn, out_fn, out_toks, per_chunk=None, after_chunk=None):
            for c in range(NCH):
                norm_stat(xg, xg_tok, bank, c)
            norm_finish(xg, xg_tok, bank, gain_fn, out_fn, out_toks, per_chunk, after_chunk)

        def x_src_ap(l, g):
            src = xT_in if l == 0 else xs
            return src.rearrange("(c p) t -> p c t", p=128)[:, :, g * TG:(g + 1) * TG]

        for l in range(L):
            P.barrier()
            ph = contextlib.ExitStack()
            cur_st[0] = ph
            if l == 0:
                xg_l = sbn("xg", [128, NCH, TG], F32, 2)
                p1banks = Rot(banks[0:2])
                for g in range(NTG):
                    xg, xg_tok = xg_l[g % 2]
                    P.dma("sp", xg[:, :, :], x_src_ap(l, g), r=([xs_tok[g]] if l > 0 else []), w=[xg_tok])
                    norm_group(xg, xg_tok, p1banks.next(), lambda c: gain_ap(l, 0, c),
                               lambda c, g=g: hT[:, c, g * TG:(g + 1) * TG], [hT_tok[g]])
                P.barrier()
            ph.close()
            if stop_at <= 1:
                break
            ph = contextlib.ExitStack()
            cur_st[0] = ph
            wq_l = sbn("wq", [128, NCH, 128], BF16, 4)
            QK_l = [(sb("QT%d" % i, [128, S], BF16), sb("KT%d" % i, [128, S], BF16)) for i in range(2)]
            Vb, Vb_tok = sb("Vb", [128, NT * 192], BF16)
            acc_l = sbn("acc", [128, S], F32, 2)
            oTp_l = sbn("oTp", [128, S], BF16, 1)
            qraw_l = sbn("qraw", [128, TG], BF16, 2)
            t1_l = sbn("t1", [128, TG], F32, 2)
            cs_l = sbn("cs", [128, 2, TG], F32, 2)
            PT_l = sbn("PT", [128, 512], BF16, 3)
            E_l = sbn("E", [128, 384], F32, 3)
            Ec_l = sbn("Ec", [128, 384], F32, 2)
            Lp_l = sbn("Lp", [128, 384], BF16, 2)
            Aw_l = sbn("Aw", [128, 384], BF16, 2)
            km32, km32_tok = sb("km32", [128, 16], F32)
            kmh, kmh_tok = sb("kmh", [128, 16], BF16)
            kml, kml_tok = sb("kml", [128, 16], BF16)
            gate_l = sbn("gate", [128, 16], F32, 2)
            m8_l = sbn("m8", [128, 8], F32, 2)
            sel_l = sbn("sel", [128, 16], F32, 2)
            nm_l = sbn("nm", [128, 16], BF16, 24)
            KZ_l = sbn("KZ", [128, S], BF16, 1)

            wq_stream = WStream(P, "sp", wq_l,
                                [(lambda t, b=b: (t[:, :, :], wqkv_b[l, b].rearrange("p (c n) -> p c n", n=128), [wtok[("qkv", l, b)]])) for b in blk_order],
                                ahead=3)
            wq_i = [0]

            def next_wq():
                r_ = wq_stream.get(wq_i[0])
                wq_i[0] += 1
                return r_

            projb = Rot(banks[0:3])
            projb2 = Rot(banks[3:5])
            bg_tasks = []

            def run_bg(n=1):
                for _ in range(n):
                    if bg_tasks:
                        bg_tasks.pop(0)()
            qraw_rot, t1_rot, t2_rot, cs_rot = Rot(qraw_l), Rot(t1_l), Rot(t2_l), Rot(cs_l)
            evac_flip = [0]

            def proj_fm(dst, dst_tok, rope):
                if os.environ.get("MK_NOROPE"):
                    rope = False
                w, w_tok = next_wq()
                pending = []
                cs_q = []

                def cs_load(g_):
                    cs, cs_tok = cs_rot.next()
                    P.dma("sp", cs[:, :, :], cf_d[:, :, g_ * TG:(g_ + 1) * TG], w=[cs_tok])
                    cs_q.append((cs, cs_tok))
                if rope:
                    cs_load(0)

                def rope_tail(bk, bk_tok, qr, qr_tok, cols):
                    bk2, bk2_tok = projb2.next()
                    P.op("pe", CALL("matmul", bk2[:, :], lhsT=perm, rhs=qr[:, :], start=True, stop=True),
                         r=[qr_tok, cb_tok], w=[bk2_tok])
                    t1, t1_tok = t1_rot.next()
                    t2, t2_tok = t2_rot.next()
                    cs, cs_tok = cs_q.pop(0)
                    gnext = cols.stop // TG
                    if gnext < NTG:
                        cs_load(gnext)
                    P.op("dve", CALL("tensor_tensor", out=t1[:, :], in0=bk[:, :], in1=cs[:, 0, :], op=ALU.mult),
                         r=[bk_tok, cs_tok], w=[t1_tok])
                    P.op("dve", CALL("tensor_tensor", out=t2[:, :], in0=bk2[:, :], in1=cs[:, 1, :], op=ALU.mult),
                         r=[bk2_tok, cs_tok], w=[t2_tok])
                    P.op("dve", CALL("tensor_tensor", out=dst[:, cols], in0=t1[:, :], in1=t2[:, :], op=ALU.add),
                         r=[t1_tok, t2_tok], w=[dst_tok])

                for g in range(NTG):
                    bk, bk_tok = projb.next()
                    cols = slice(g * TG, (g + 1) * TG)
                    for c in range(NCH):
                        P.op("pe", CALL("matmul", bk[:, :], lhsT=w[:, c, :], rhs=hT[:, c, cols],
                                        start=(c == 0), stop=(c == NCH - 1)),
                             r=[w_tok, hT_tok[g]], w=[bk_tok])
                    if not rope:
                        evac_flip[0] = 1
                        if evac_flip[0]:
                            P.op("act", CALL("activation", out=dst[:, cols], in_=bk[:, :], func=AF.Copy),
                                 r=[bk_tok], w=[dst_tok])
                        else:
                            P.op("dve", CALL("tensor_copy", out=dst[:, cols], in_=bk[:, :]),
                                 r=[bk_tok], w=[dst_tok])
                    else:
                        qr, qr_tok = qraw_rot.next()
                        P.op("act", CALL("activation", out=qr[:, :], in_=bk[:, :], func=AF.Copy),
                             r=[bk_tok], w=[qr_tok])
                        if pending:
                            rope_tail(*pending.pop(0))
                        pending.append((bk, bk_tok, qr, qr_tok, cols))
                    run_bg(1)
                while pending:
                    rope_tail(*pending.pop(0))

            def proj_v(tile_tok_slices, out_view_fn, in_view_fn):
                w, w_tok = next_wq()
                for quad in range(NT // 4):
                    bk, bk_tok = projb.next()
                    for j in range(4):
                        t = quad * 4 + j
                        gs, sl = tile_tok_slices[t]
                        for c in range(NCH):
                            P.op("pe", CALL("matmul", bk[:, j * 128:(j + 1) * 128], lhsT=hT[:, c, sl], rhs=w[:, c, :],
                                                                                 start=(c == 0), stop=(c == NCH - 1)),
                                 r=[w_tok] + [hT_tok[g] for g in gs], w=[bk_tok])
                    evac_flip[0] = 1
                    if evac_flip[0]:
                        P.op("act", CALL("activation", out=out_view_fn(quad), in_=in_view_fn(bk), func=AF.Copy),
                             r=[bk_tok], w=[Vb_tok])
                    else:
                        P.op("dve", CALL("tensor_copy", out=out_view_fn(quad), in_=in_view_fn(bk)),
                             r=[bk_tok], w=[Vb_tok])
                    run_bg(1)

            def tok_slice(d, r, n):
                start = r + d * 128 * n
                return slice(start, start + 127 * d + 1, d)

            def groups_of(sl):
                return sorted(set([sl.start // TG, (sl.stop - 1) // TG]) | set(range(sl.start // TG, (sl.stop - 1) // TG + 1)))

            PT_rot = Rot(PT_l)
            qk_i = 0
            oTp, oTp_tok = oTp_l[0]

            for ui, (kind, idx) in enumerate(units):
                if stop_at <= ui + 1:
                    break
                if kind == "A":
                    VpA = Vb[:, :].rearrange("p (t n) -> p t n", n=192)
                    if idx == 0:
                        P.op("dve", CALL("memset", VpA[:, :, 64:128], 1.0), w=[Vb_tok])
                    Sb_rot = Rot(banks[3:6])
                    Ob_rot = Rot(banks[6:8])
                    for gi, d in enumerate(DIL):
                        (QT, QT_tok), (KT, KT_tok) = QK_l[qk_i % 2]
                        qk_i += 1
                        proj_fm(QT, QT_tok, True)
                        chk(1)
                        proj_fm(KT, KT_tok, True)
                        chk(2)
                        nb = NT // d
                        tiles = []
                        for r in range(d):
                            for n in range(nb):
                                sl = tok_slice(d, r, n)
                                tiles.append((groups_of(sl), sl))
                        proj_v(tiles,
                               lambda quad: VpA[:, quad * 4:(quad + 1) * 4, :].rearrange("p t (h c) -> p t h c", c=64)[:, :, 0:3:2, :],
                               lambda bk: bk[:, :].rearrange("p (t h c) -> p t h c", t=4, h=2))
                        chk(3)
                        run_bg(100)
                        quads = []
                        if d == 1:
                            for m in range(8):
                                quads.append(([(0, 4 * m + j) for j in range(4)],
                                              lambda a, m=m: a[:, 512 * m:512 * (m + 1)].rearrange("p (b i) -> p b i", i=128)))
                        elif d == 4:
                            for n in range(8):
                                quads.append(([(r, n) for r in range(4)],
                                              lambda a, n=n: a[:, 512 * n:512 * (n + 1)].rearrange("p (i r) -> p r i", r=4)))
                        else:
                            for n in range(2):
                                for a4 in range(4):
                                    quads.append(([(4 * a4 + j, n) for j in range(4)],
                                                  lambda a, n=n, a4=a4: a[:, 2048 * n:2048 * (n + 1)].rearrange("p (i r) -> p r i", r=16)[:, 4 * a4:4 * a4 + 4, :]))
                        for hh in range(2):
                            hp = slice(hh * 64, (hh + 1) * 64)
                            acc, acc_tok = acc_l[hh]
                            jobs = []
                            for blocks, accview in quads:
                                ob, ob_tok = Ob_rot.next()
                                for j, (r, n) in enumerate(blocks):
                                    jobs.append((r, n, j, ob, ob_tok, accview if j == 3 else None))
                            pipe = [None, None]
                            for job in jobs + [None, None]:
                                cur = None
                                prev = pipe[0]
                                if job is not None:
                                    r, n, j, ob, ob_tok, accview = job
                                    sbk, sbk_tok = Sb_rot.next()
                                    qsl = tok_slice(d, r, n)
                                    c0 = 0 if n > 0 else 128
                                    if n > 0:
                                        psl = tok_slice(d, r, n - 1)
                                        P.op("pe", CALL("matmul", sbk[:, 0:128], lhsT=KT[hp, psl], rhs=QT[hp, qsl], start=True, stop=True),
                                             r=[KT_tok, QT_tok], w=[sbk_tok])
                                    P.op("pe", CALL("matmul", sbk[:, 128:256], lhsT=KT[hp, qsl], rhs=QT[hp, qsl], start=True, stop=True),
                                         r=[KT_tok, QT_tok], w=[sbk_tok])
                                    pt, pt_tok = PT_rot.next()
                                    P.op("act", CALL("activation", out=pt[:, c0:256], in_=sbk[:, c0:256], func=AF.Exp, scale=0.125),
                                         r=[sbk_tok], w=[pt_tok])
                                    P.op("dve", CALL("tensor_tensor", out=pt[:, c0:256], in0=pt[:, c0:256], in1=m01A[:, c0:256], op=ALU.mult),
                                         r=[pt_tok, cb_tok], w=[pt_tok])
                                    cur = (r, n, j, ob, ob_tok, accview, pt, pt_tok)
                                if prev is not None:
                                    r, n, j, ob, ob_tok, accview, pt, pt_tok = prev
                                    tcur = r * nb + n
                                    vc = slice(hh * 64, hh * 64 + 128)
                                    if n > 0:
                                        P.op("pe", CALL("matmul", ob[:, j * 128:(j + 1) * 128], lhsT=VpA[:, tcur - 1, vc], rhs=pt[:, 0:128], start=True, stop=False),
                                             r=[Vb_tok, pt_tok], w=[ob_tok])
                                    P.op("pe", CALL("matmul", ob[:, j * 128:(j + 1) * 128], lhsT=VpA[:, tcur, vc], rhs=pt[:, 128:256], start=(n == 0), stop=True),
                                         r=[Vb_tok, pt_tok], w=[ob_tok])
                                    if accview is not None:
                                        obv = ob[:, :].rearrange("p (b i) -> p b i", i=128)
                                        if gi == 0:
                                            P.op("dve", CALL("tensor_copy", out=accview(acc), in_=obv),
                                                 r=[ob_tok], w=[acc_tok])
                                        else:
                                            P.op("dve", CALL("tensor_tensor", out=accview(acc), in0=obv, in1=accview(acc), op=ALU.add),
                                                 r=[ob_tok, acc_tok], w=[acc_tok])
                                pipe = [pipe[1], cur]
                            chk(4 + hh + 2 * gi)
                    for hh in range(2):
                        for g in range(NTG):
                            def fin(hh=hh, g=g):
                                acc, acc_tok = acc_l[hh]
                                cols = slice(g * TG, (g + 1) * TG)
                                t1, t1_tok = t1_rot.next()
                                nump = slice(hh * 64, (hh + 1) * 64)
                                denp = slice((1 - hh) * 64, (2 - hh) * 64)
                                P.op("act", CALL("activation", out=t1[nump, :], in_=acc[denp, cols], func=AF.Ln),
                                     r=[acc_tok], w=[t1_tok])
                                P.op("act", CALL("activation", out=t1[nump, :], in_=t1[nump, :], func=AF.Exp, scale=-1.0),
                                     r=[t1_tok], w=[t1_tok])
                                P.op("dve", CALL("tensor_tensor", out=oTp[nump, cols], in0=acc[nump, cols], in1=t1[nump, :], op=ALU.mult),
                                     r=[acc_tok, t1_tok], w=[oTp_tok])
                            bg_tasks.append(fin)

                elif kind == "B":
                    VpA = Vb[:, :].rearrange("p (t n) -> p t n", n=192)
                    (QT, QT_tok), (KT, KT_tok) = QK_l[qk_i % 2]
                    qk_i += 1
                    proj_fm(QT, QT_tok, True)
                    proj_fm(KT, KT_tok, True)
                    run_bg(100)
                    QZ = [acc_l[h_][0][:, :].bitcast(BF16)[:, 0:S] for h_ in range(2)]
                    QZ_tok = [acc_l[h_][1] for h_ in range(2)]
                    KZ = [KZ_l[0][0], KT]
                    KZ_tok = [KZ_l[0][1], KT_tok]
                    gsl = [slice(64, 80), slice(0, 16)]
                    if idx == 0:
                        P.op("dve", CALL("memset", QZ[0][64:128, :], 0.0), w=[QZ_tok[0]])
                        P.op("dve", CALL("memset", QZ[1][0:64, :], 0.0), w=[QZ_tok[1]])
                        P.op("dve", CALL("memset", KZ[0][64:128, :], 0.0), w=[KZ_tok[0]])
                        for g in range(NTG):
                            cols = slice(g * TG, (g + 1) * TG)
                            t1, t1_tok = t1_rot.next()
                            P.dma("sp", t1[0:16, :], ind_d[:, cols], w=[t1_tok])
                            P.op("dve", CALL("tensor_copy", out=KZ[0][64:80, cols], in_=t1[0:16, :]), r=[t1_tok], w=[KZ_tok[0]])
                    P.op("dve", CALL("tensor_copy", out=QZ[0][0:64, :], in_=QT[0:64, :]), r=[QT_tok], w=[QZ_tok[0]])
                    P.op("dve", CALL("tensor_copy", out=QZ[1][64:128, :], in_=QT[64:128, :]), r=[QT_tok], w=[QZ_tok[1]])
                    tiles = [([t // 4], slice(t * 128, (t + 1) * 128)) for t in range(NT)]
                    proj_v(tiles,
                           lambda quad: VpA[:, quad * 4:(quad + 1) * 4, :].rearrange("p t (h c) -> p t h c", c=64)[:, :, 0:3:2, :],
                           lambda bk: bk[:, :].rearrange("p (t h c) -> p t h c", t=4, h=2))
                    run_bg(100)
                    P.op("dve", CALL("tensor_reduce", out=km32[:, :], in_=KT[:, :].rearrange("p (n k) -> p n k", k=256), axis=AX.X, op=ALU.add),
                         r=[KT_tok], w=[km32_tok])
                    P.op("dve", CALL("tensor_copy", out=kmh[:, :], in_=km32[:, :]), r=[km32_tok], w=[kmh_tok])
                    P.op("dve", CALL("tensor_tensor", out=kml[:, :], in0=km32[:, :], in1=kmh[:, :], op=ALU.subtract),
                         r=[km32_tok, kmh_tok], w=[kml_tok])
                    P.op("dve", CALL("tensor_copy", out=KZ[0][0:64, :], in_=KT[0:64, :]), r=[KT_tok], w=[KZ_tok[0]])
                    P.op("dve", CALL("memset", KT[0:64, :], 0.0), w=[KT_tok])
                    P.op("dve", CALL("tensor_copy", out=KT[0:16, :], in_=KZ[0][64:80, :]), r=[KZ_tok[0]], w=[KT_tok])
                    Sb_rot = Rot(banks[0:3])
                    Ob_rot = Rot(banks[3:5])
                    gbk, gbk_tok = banks[5]
                    tbk, tbk_tok = banks[6]
                    m8_rot, sel_rot, nm_rot = Rot(m8_l), Rot(sel_l), Rot(nm_l)
                    for hh in range(2):
                        gt, gt_tok = gate_l[hh]
                        P.op("dve", CALL("memset", gt[:, :], -1e30), w=[gt_tok])
                    for (nm, nm_tok) in nm_l:
                        P.op("dve", CALL("memset", nm[:, :], 0.0), w=[nm_tok])

                    nm_of = {}

                    def emit_gate_a(G):
                        for hh in range(2):
                            hp = slice(hh * 64, (hh + 1) * 64)
                            for i in range(4):
                                qt = 4 * G + i
                                own = qt // 2
                                qsl = slice(qt * 128, (qt + 1) * 128)
                                nm, nm_tok = nm_rot.next()
                                nm_of[(G, hh, i)] = (nm, nm_tok)
                                if own >= 4:
                                    P.op("pe", CALL("matmul", gbk[:, hh * 16:hh * 16 + 16], lhsT=QT[hp, qsl], rhs=kmh[hp, :], start=True, stop=False),
                                         r=[QT_tok, kmh_tok], w=[gbk_tok])
                                    P.op("pe", CALL("matmul", gbk[:, hh * 16:hh * 16 + 16], lhsT=QT[hp, qsl], rhs=kml[hp, :], start=False, stop=True),
                                         r=[QT_tok, kml_tok], w=[gbk_tok])
                                    gt, gt_tok = gate_l[hh]
                                    P.op("dve", CALL("tensor_copy", out=gt[:, 0:own], in_=gbk[:, hh * 16:hh * 16 + own]),
                                         r=[gbk_tok], w=[gt_tok])
                                    m8, m8_tok = m8_rot.next()
                                    P.op("dve", CALL("max", out=m8[:, :], in_=gt[:, :]), r=[gt_tok], w=[m8_tok])
                                    sel, sel_tok = sel_rot.next()
                                    P.op("dve", CALL("tensor_scalar", out=sel[:, :], in0=gt[:, :], scalar1=m8[:, 2:3], scalar2=None, op0=ALU.is_ge),
                                         r=[gt_tok, m8_tok], w=[sel_tok])
                                    P.op("dve", CALL("tensor_scalar", out=nm[:, 0:own], in0=sel[:, 0:own], scalar1=-NEG, scalar2=NEG, op0=ALU.mult, op1=ALU.add),
                                         r=[sel_tok], w=[nm_tok])

                    def emit_gate_b(G):
                        for hh in range(2):
                            for i in range(4):
                                nm, nm_tok = nm_of.pop((G, hh, i))
                                P.op("pe", CALL("matmul", tbk[0:16, i * 128:(i + 1) * 128], lhsT=nm[:, :], rhs=ident, start=True, stop=True),
                                     r=[nm_tok, cb_tok], w=[tbk_tok])
                            P.op("act", CALL("activation", out=QZ[hh][gsl[hh], G * TG:(G + 1) * TG], in_=tbk[0:16, :], func=AF.Copy),
                                 r=[tbk_tok], w=[QZ_tok[hh]])

                    emit_gate_a(0)
                    emit_gate_b(0)
                    emit_gate_a(1)
                    for G in range(NTG):
                        if G >= 1:
                            emit_gate_b(G)
                            if G + 1 < NTG:
                                emit_gate_a(G + 1)
                        for hh in range(2):
                            nk = 4 * G + 4
                            ob, ob_tok = Ob_rot.next()
                            vc = slice(hh * 64, hh * 64 + 128)
                            pipe = [None, None]
                            for kt in list(range(nk)) + [None, None]:
                                cur = None
                                prev = pipe[0]
                                if kt is not None:
                                    j = kt - 4 * G
                                    c0 = max(j, 0) * 128
                                    ksl = slice(kt * 128, (kt + 1) * 128)
                                    sbk, sbk_tok = Sb_rot.next()
                                    P.op("pe", CALL("matmul", sbk[:, c0:512], lhsT=KZ[hh][:, ksl], rhs=QZ[hh][:, G * TG + c0:(G + 1) * TG],
                                                    start=True, stop=(j < 0)),
                                         r=[KZ_tok[hh], QZ_tok[hh]], w=[sbk_tok])
                                    if j >= 0:
                                        P.op("pe", CALL("matmul", sbk[:, c0:c0 + 128], lhsT=ident, rhs=maskLE, start=False, stop=True),
                                             r=[cb_tok], w=[sbk_tok])
                                    pt, pt_tok = PT_rot.next()
                                    P.op("act", CALL("activation", out=pt[:, c0:512], in_=sbk[:, c0:512], func=AF.Exp, scale=0.125),
                                         r=[sbk_tok], w=[pt_tok])
                                    cur = (kt, c0, pt, pt_tok)
                                if prev is not None:
                                    kt2, c02, pt, pt_tok = prev
                                    P.op("pe", CALL("matmul", ob[:, c02:512], lhsT=VpA[:, kt2, vc], rhs=pt[:, c02:512],
                                                    start=(kt2 == 0), stop=(kt2 == nk - 1)),
                                         r=[pt_tok, Vb_tok], w=[ob_tok])
                                pipe = [pipe[1], cur]
                            cols = slice(G * TG, (G + 1) * TG)
                            nump = slice(hh * 64, (hh + 1) * 64)
                            denp = slice((1 - hh) * 64, (2 - hh) * 64)
                            t1, t1_tok = t1_rot.next()
                            P.op("dve", CALL("reciprocal", out=t1[nump, :], in_=ob[denp, :]), r=[ob_tok], w=[t1_tok])
                            P.op("dve", CALL("tensor_tensor", out=oTp[nump, cols], in0=ob[nump, :], in1=t1[nump, :], op=ALU.mult),
                                 r=[ob_tok, t1_tok], w=[oTp_tok])

                else:
                    VpC = Vb[:, 0:NT * 128].rearrange("p (t n) -> p t n", n=128)
                    (QT, QT_tok), (KT, KT_tok) = QK_l[qk_i % 2]
                    qk_i += 1
                    proj_fm(QT, QT_tok, False)
                    proj_fm(KT, KT_tok, False)
                    run_bg(100)
                    QZ = [acc_l[h_][0][:, :].bitcast(BF16)[:, 0:S] for h_ in range(2)]
                    QZ_tok = [acc_l[h_][1] for h_ in range(2)]
                    if idx == 0:
                        P.op("dve", CALL("memset", QZ[0][64:80, :], 0.0), w=[QZ_tok[0]])
                        P.op("dve", CALL("memset", QZ[1][0:16, :], 0.0), w=[QZ_tok[1]])
                    P.op("dve", CALL("tensor_copy", out=QZ[0][0:64, :], in_=QT[0:64, :]), r=[QT_tok], w=[QZ_tok[0]])
                    P.op("dve", CALL("tensor_copy", out=QZ[1][64:128, :], in_=QT[64:128, :]), r=[QT_tok], w=[QZ_tok[1]])
                    tiles = [([t // 4], slice(t * 128, (t + 1) * 128)) for t in range(NT)]
                    proj_v(tiles,
                           lambda quad: VpC[:, quad * 4:(quad + 1) * 4, :],
                           lambda bk: bk[:, :].rearrange("p (t n) -> p t n", t=4))
                    Z_rot = Rot(banks[0:2])
                    C_rot = Rot(banks[2:4])
                    O_rot = Rot(banks[4:6])
                    E_rot, Lp_rot, Aw_rot, Ec_rot = Rot(E_l), Rot(Lp_l), Rot(Aw_l), Rot(Ec_l)
                    jobs = [(qt, hh) for qt in range(NT) for hh in range(2)]
                    s1 = None
                    s2 = None
                    obank = None
                    for job in jobs + [None, None]:
                        c1 = None
                        if job is not None:
                            qt, hh = job
                            hp = slice(hh * 64, (hh + 1) * 64)
                            qsl = slice(qt * 128, (qt + 1) * 128)
                            kts = [kt for kt in range(max(0, qt - SB_WIN + 1), qt + 1)]
                            zb, zb_tok = Z_rot.next()
                            for j, kt in enumerate(kts):
                                ksl = slice(kt * 128, (kt + 1) * 128)
                                P.op("pe", CALL("matmul", zb[:, j * 128:(j + 1) * 128], lhsT=KT[:, ksl], rhs=QZ[hh][:, qsl], start=True, stop=True),
                                     r=[KT_tok, QZ_tok[hh]], w=[zb_tok])
                            ncol = len(kts) * 128
                            E, E_tok = E_rot.next()
                            Lp, Lp_tok = Lp_rot.next()
                            P.op("act", CALL("activation", out=E[:, 0:ncol], in_=zb[:, 0:ncol], func=AF.Exp, scale=0.125),
                                 r=[zb_tok], w=[E_tok])
                            P.op("dve", CALL("tensor_tensor", out=E[:, ncol - 128:ncol], in0=E[:, ncol - 128:ncol], in1=m01S, op=ALU.mult),
                                 r=[E_tok, cb_tok], w=[E_tok])
                            P.op("act", CALL("activation", out=Lp[:, 0:ncol], in_=E[:, 0:ncol], func=AF.Ln, bias=1.0, scale=1.0),
                                 r=[E_tok], w=[Lp_tok])
                            c1 = (qt, hh, kts, Lp, Lp_tok, E, E_tok)
                        c2 = None
                        if s1 is not None:
                            qt, hh, kts, Lp, Lp_tok, E, E_tok = s1
                            hp = slice(hh * 64, (hh + 1) * 64)
                            qsl = slice(qt * 128, (qt + 1) * 128)
                            cbk, cbk_tok = C_rot.next()
                            nk = len(kts)
                            for j, kt in enumerate(kts):
                                ksl = slice(kt * 128, (kt + 1) * 128)
                                reg = slice(j * 128, (j + 1) * 128)
                                P.op("pe", CALL("matmul", cbk[:, reg], lhsT=uincl, rhs=Lp[:, j * 128:(j + 1) * 128], start=True, stop=(j == nk - 1)),
                                     r=[Lp_tok, cb_tok], w=[cbk_tok])
                                for j2 in range(j + 1, nk):
                                    P.op("pe", CALL("matmul", cbk[:, reg], lhsT=negones, rhs=Lp[:, j2 * 128:(j2 + 1) * 128], start=False, stop=(j2 == nk - 1)),
                                         r=[Lp_tok, cb_tok], w=[cbk_tok])
                            Aw, Aw_tok = Aw_rot.next()
                            Ec, Ec_tok = Ec_rot.next()
                            ncol = nk * 128
                            P.op("act", CALL("activation", out=Ec[:, 0:ncol], in_=cbk[:, 0:ncol], func=AF.Exp, scale=0.125),
                                 r=[cbk_tok], w=[Ec_tok])
                            P.op("dve", CALL("tensor_tensor", out=Aw[:, 0:ncol], in0=E[:, 0:ncol], in1=Ec[:, 0:ncol], op=ALU.mult),
                                 r=[E_tok, Ec_tok], w=[Aw_tok])
                            c2 = (qt, hh, kts, Aw, Aw_tok)
                        if s2 is not None:
                            qt, hh, kts, Aw, Aw_tok = s2
                            hp = slice(hh * 64, (hh + 1) * 64)
                            q4 = qt % 4
                            if q4 == 0 and hh == 0:
                                obank = O_rot.next()
                            ob, ob_tok = obank
                            nk = len(kts)
                            for j, kt in enumerate(kts):
                                P.op("pe", CALL("matmul", ob[hp, q4 * 128:(q4 + 1) * 128], lhsT=VpC[:, kt, hp], rhs=Aw[:, j * 128:(j + 1) * 128],
                                                                                                         start=(j == 0), stop=(j == nk - 1)),
                                     r=[Aw_tok, Vb_tok], w=[ob_tok])
                            if q4 == 3 and hh == 1:
                                cols = slice((qt - 3) * 128, (qt + 1) * 128)
                                P.op("dve", CALL("tensor_copy", out=oTp[:, cols], in_=ob[:, :]),
                                     r=[ob_tok], w=[oTp_tok])
                        s2 = c2
                        s1 = c1

                chk(20)
                if ui == 0 and l == 0:
                    conv_layer(0, 1, after=[P.ops["pe"][-1]])
                if ui == 1 and l + 1 < L:
                    conv_layer(l + 1, 0, after=[P.ops["pe"][-1]])
                    conv_layer(l + 1, 1)
                def out_dma(ui=ui):
                    P.dma("sp", oT[ui * 128:(ui + 1) * 128, :], oTp[:, :], r=[oTp_tok], w=[oT_tok[ui]])
                bg_tasks.append(out_dma)
            run_bg(100)

            P.barrier()
            ph.close()
            if stop_at <= 9:
                break
            ph = contextlib.ExitStack()
            cur_st[0] = ph
            xg_l = sbn("xg", [128, NCH, TG], F32, 1)
            og_l = sbn("og", [128, NCH, TG], BF16, 1)
            sg_l = sbn("sg", [128, TG], F32, 3)
            mt_l = sbn("mt", [128, TG], F32, 2)
            mg, mg_tok = sb("mg", [128, NCH, TG], BF16)
            h2 = mg
            rl_l = sbn("rl", [128, TG], F32, 2)
            uT, uT_tok = sb("uT", [128, 32, TG], BF16)
            wg_l = sbn("wg", [128, NCH, 3, 128], BF16, 2)
            wo3_l = sbn("wo3", [128, NCH, 128], BF16, 2)
            wo4_l = sbn("wo4", [128, NCH, 128], BF16, 4)
            wup_l = sbn("wupb", [128, NCH, 256], BF16, 3)
            wdn_l = sbn("wdnb", [128, 2, 1024], BF16, 3)
            fo_l = rl_l
            wg_stream = WStream(P, "sp", wg_l,
                                [(lambda t, cb_=cb_: (t[:, :, :, :], wgate_b[l, cb_].rearrange("p (c b n) -> p c b n", b=3, n=128), [wtok[("gate", l, cb_)]]))
                                 for g in range(NTG) for cb_ in range(8)], ahead=1)
            wo3_stream = WStream(P, "sp", wo3_l,
                                 [(lambda t, cb_=cb_: (t[:, :, :], wout_b[l, cb_].rearrange("p (c n) -> p c n", n=128), [wtok[("out", l, cb_)]]))
                                  for g in range(NTG) for cb_ in range(8)], ahead=1)
            wo4_stream = WStream(P, "sp", wo4_l,
                                 [(lambda t, cb_=cb_: (t[:, :, :], wo_b[l, cb_].rearrange("p (c n) -> p c n", n=128), [wtok[("o", l, cb_)]]))
                                  for g in range(NTG) for cb_ in range(8)], ahead=3)
            wup_stream = WStream(P, "sp", wup_l,
                                 [(lambda t, j4=j4: (t[:, :, :], wup_b[l, j4].rearrange("p (c n) -> p c n", n=256), [wtok[("up", l, j4)]]))
                                  for g in range(NTG) for j4 in range(16)], ahead=2)
            wdn_stream = WStream(P, "sp", wdn_l,
                                 [(lambda t, j4=j4: (t[:, :, :], wdn_b[l, j4].rearrange("p (j n) -> p j n", n=1024), [wtok[("dn", l, j4)]]))
                                  for g in range(NTG) for j4 in range(16)], ahead=2)
            allb = Rot(banks[0:7])
            statb = banks[7]
            sg_rot, mt_rot, rl_rot = Rot(sg_l), Rot(mt_l), Rot(rl_l)
            last_layer = (l == L - 1)
            h2_toks = [P.buf("h2c%d" % c) for c in range(NCH)]
            tail = [None]
            for g in range(NTG):
                cols = slice(g * TG, (g + 1) * TG)
                xg, xg_tok = xg_l[0]
                og, og_tok = og_l[0]
                if g == 0:
                    P.dma("sp", og[:, :, :], oT.rearrange("(c p) t -> p c t", p=128)[:, :, cols], r=oT_tok, w=[og_tok])
                xload = tail[0] is not None
                if not xload:
                    P.dma("sp", xg[:, :, :], x_src_ap(l, g), r=([xs_tok[g]] if l > 0 else []), w=[xg_tok])
                for cb_ in range(8):
                    if cb_ == 1 and tail[0] is not None:
                        tail[0]()
                        tail[0] = None
                    if cb_ == 5 and xload:
                        P.dma("sp", xg[:, :, :], x_src_ap(l, g), r=([xs_tok[g]] if l > 0 else []), w=[xg_tok])
                    wg_, wg_tok = wg_stream.get(g * 8 + cb_)
                    w3, w3_tok = wo3_stream.get(g * 8 + cb_)
                    mt, mt_tok = mt_rot.next()
                    for br, (c0, c1) in enumerate(((0, 2), (2, 5), (5, 8))):
                        gb, gb_tok = allb.next()
                        for c in range(NCH):
                            P.op("pe", CALL("matmul", gb[:, :], lhsT=wg_[:, c, br, :], rhs=hT[:, c, cols], start=(c == 0), stop=(c == NCH - 1)),
                                 r=[wg_tok, hT_tok[g]], w=[gb_tok])
                        sg, sg_tok = sg_rot.next()
                        P.op("act", CALL("activation", out=sg[:, :], in_=gb[:, :], func=AF.Sigmoid),
                             r=[gb_tok], w=[sg_tok])
                        yb, yb_tok = allb.next()
                        for c in range(c0, c1):
                            P.op("pe", CALL("matmul", yb[:, :], lhsT=w3[:, c, :], rhs=og[:, c, :], start=(c == c0), stop=(c == c1 - 1)),
                                 r=[w3_tok, og_tok], w=[yb_tok])
                        if br == 0:
                            P.op("dve", CALL("tensor_tensor", out=mt[:, :], in0=yb[:, :], in1=sg[:, :], op=ALU.mult),
                                 r=[yb_tok, sg_tok], w=[mt_tok])
                        else:
                            t2, t2_tok = t2_rot.next()
                            P.op("dve", CALL("tensor_tensor", out=t2[:, :], in0=yb[:, :], in1=sg[:, :], op=ALU.mult),
                                 r=[yb_tok, sg_tok], w=[t2_tok])
                            if br == 1:
                                P.op("dve", CALL("tensor_tensor", out=mt[:, :], in0=mt[:, :], in1=t2[:, :], op=ALU.add),
                                     r=[mt_tok, t2_tok], w=[mt_tok])
                            else:
                                P.op("dve", CALL("tensor_tensor", out=mg[:, cb_, :], in0=mt[:, :], in1=t2[:, :], op=ALU.add),
                                     r=[mt_tok, t2_tok], w=[h2_toks[cb_]])
                for cb_ in range(8):
                    w4, w4_tok = wo4_stream.get(g * 8 + cb_)
                    rb, rb_tok = allb.next()
                    for c in range(NCH):
                        P.op("pe", CALL("matmul", rb[:, :], lhsT=w4[:, c, :], rhs=mg[:, c, :], start=(c == 0), stop=(c == NCH - 1)),
                             r=[w4_tok, h2_toks[c]], w=[rb_tok])
                    P.op("dve", CALL("tensor_tensor", out=xg[:, cb_, :], in0=rb[:, :], in1=xg[:, cb_, :], op=ALU.add),
                         r=[rb_tok, xg_tok], w=[xg_tok])
                    if cb_ >= 2:
                        norm_stat(xg, xg_tok, statb, cb_ - 2)
                    if cb_ == 0 and g + 1 < NTG:
                        ncols = slice((g + 1) * TG, (g + 2) * TG)
                        P.dma("sp", og[:, :, :], oT.rearrange("(c p) t -> p c t", p=128)[:, :, ncols], r=oT_tok, w=[og_tok])
                norm_stat(xg, xg_tok, statb, NCH - 2)
                norm_stat(xg, xg_tok, statb, NCH - 1)
                norm_finish(xg, xg_tok, statb, lambda c: gain_ap(l, 1, c), lambda c: h2[:, c, :], None, per_chunk=lambda c: h2_toks[c])
                for j4 in range(16):
                    wu, wu_tok = wup_stream.get(g * 16 + j4)
                    for jj in range(2):
                        jb = j4 * 2 + jj
                        ub, ub_tok = allb.next()
                        for c in range(NCH):
                            P.op("pe", CALL("matmul", ub[:, :], lhsT=wu[:, c, jj * 128:(jj + 1) * 128], rhs=h2[:, c, :], start=(c == 0), stop=(c == NCH - 1)),
                                 r=[wu_tok, h2_toks[c]], w=[ub_tok])
                        rl, rl_tok = rl_rot.next()
                        P.op("act", CALL("activation", out=rl[:, :], in_=ub[:, :], func=AF.Relu),
                             r=[ub_tok], w=[rl_tok])
                        P.op("dve", CALL("tensor_tensor", out=uT[:, jb, :], in0=rl[:, :], in1=rl[:, :], op=ALU.mult),
                             r=[rl_tok], w=[uT_tok])
                for j4 in range(16):
                    wd, wd_tok = wdn_stream.get(g * 16 + j4)
                    for jj in range(2):
                        jc = j4 * 2 + jj
                        for cb_ in range(8):
                            db, db_tok = banks[cb_]
                            P.op("pe", CALL("matmul", db[:, :], lhsT=wd[:, jj, cb_ * 128:(cb_ + 1) * 128], rhs=uT[:, jc, :], start=(jc == 0), stop=(jc == 31)),
                                 r=[wd_tok, uT_tok], w=[db_tok])
                for cb_ in range(8):
                    db, db_tok = banks[cb_]
                    P.op("dve", CALL("tensor_tensor", out=xg[:, cb_, :], in0=db[:, :], in1=xg[:, cb_, :], op=ALU.add),
                         r=[db_tok, xg_tok], w=[xg_tok])
                if last_layer and final_norm:
                    def tail_fn(xg=xg, xg_tok=xg_tok, cols=cols):
                        fo_rot = Rot(fo_l)
                        fo_cur = {}

                        def fo_out(c):
                            if c not in fo_cur:
                                fo_cur[c] = fo_rot.next()
                            return fo_cur[c][0][:, :]
                        for c in range(NCH):
                            norm_stat(xg, xg_tok, statb, c)
                        norm_finish(xg, xg_tok, statb, gain_final, fo_out, None, per_chunk=lambda c: fo_cur[c][1],
                                    after_chunk=lambda c: out_dmas.append(
                                        P.dma("sp", outT[c * 128:(c + 1) * 128, cols], fo_cur[c][0][:, :], r=[fo_cur[c][1]])))
                    tail[0] = tail_fn
                elif last_layer:
                    out_dmas.append(P.dma("sp", outT.rearrange("(c p) t -> p c t", p=128)[:, :, cols], xg[:, :, :], r=[xg_tok]))
                else:
                    P.dma("sp", xs.rearrange("(c p) t -> p c t", p=128)[:, :, cols], xg[:, :, :], r=[xg_tok], w=[xs_tok[g]])

                    def tail_fn(xg=xg, xg_tok=xg_tok, cols=cols, g=g):
                        for c in range(NCH):
                            norm_stat(xg, xg_tok, statb, c)
                        norm_finish(xg, xg_tok, statb, lambda c: gain_ap(l + 1, 0, c), lambda c: hT[:, c, cols], [hT_tok[g]])
                    tail[0] = tail_fn
            if tail[0] is not None:
                tail[0]()
                tail[0] = None
            P.barrier()
            ph.close()
            cur_st[0] = st

        STOPPED[0] = False
        if not out_dmas:
            cur_st[0] = st
            out_dmas.append(P.dma("sp", outT[0:128, 0:200], gains[:, :], r=[gains_tok]))
        P.emit(final_wait_ops=out_dmas)
    return nc


def _consts():
    half = 8
    inv_freq = np.exp(-np.log(np.float32(500000.0)) * np.arange(0, 16, 2, dtype=np.float32) / np.float32(16)).astype(np.float32)
    pos = np.arange(S, dtype=np.float32)
    ang = (pos[:, None] * inv_freq[None, :]).astype(np.float32)
    cos = np.cos(ang).astype(np.float32).T
    sin = np.sin(ang).astype(np.float32).T
    cf = np.zeros((128, 2, S), np.float32)
    for hb in (0, 64):
        cf[hb:hb + 64, 0, :] = 1.0
        cf[hb:hb + 8, 0, :] = cos
        cf[hb + 8:hb + 16, 0, :] = cos
        cf[hb:hb + 8, 1, :] = -sin
        cf[hb + 8:hb + 16, 1, :] = sin
    cb = np.zeros((128, 1408), np.float32)
    cb[:, 0:128] = np.eye(128, dtype=np.float32)
    pm = np.zeros((128, 128), np.float32)
    for hb in (0, 64):
        for m in range(8):
            pm[hb + m + 8, hb + m] = 1.0
            pm[hb + m, hb + m + 8] = 1.0
    cb[:, 128:256] = pm
    k = np.arange(128)[:, None]
    q = np.arange(128)[None, :]
    cb[:, 256:384] = np.where(k >= q, 0.0, NEG)
    cb[:, 384:512] = np.where(k <= q, 0.0, NEG)
    cb[:, 512:640] = np.where(k < q, 0.0, NEG)
    cb[:, 640:768] = np.where(k >= q, -8.0, 0.0)
    cb[:, 768:896] = -8.0
    cb[:, 896:1024] = np.where(k >= q, 1.0, 0.0)
    cb[:, 1024:1152] = np.where(k <= q, 1.0, 0.0)
    cb[:, 1152:1280] = np.where(k < q, 1.0, 0.0)
    cb[:, 1280:1408] = 1.0 / D
    return cf, cb


def _layout_weights(w_in, w_out_a, w_out_b, w_out_c, w_o, w_up, w_down, L):
    w_in = np.asarray(w_in)[:L]
    wqkv = np.ascontiguousarray(w_in[:, :, :4608].reshape(L, 8, 128, 36, 128).transpose(0, 3, 2, 1, 4)).reshape(L, 36, 128, 8 * 128)
    wgate = np.ascontiguousarray(w_in[:, :, 4608:].reshape(L, 8, 128, 3, 8, 128).transpose(0, 4, 2, 1, 3, 5)).reshape(L, 8, 128, 8 * 3 * 128)
    wcat = np.concatenate([np.asarray(w_out_a)[:L], np.asarray(w_out_b)[:L], np.asarray(w_out_c)[:L]], axis=1)
    wout = np.ascontiguousarray(wcat.reshape(L, 8, 128, 8, 128).transpose(0, 3, 2, 1, 4)).reshape(L, 8, 128, 8 * 128)
    wo = np.ascontiguousarray(np.asarray(w_o)[:L].reshape(L, 8, 128, 8, 128).transpose(0, 3, 2, 1, 4)).reshape(L, 8, 128, 8 * 128)
    wup = np.ascontiguousarray(np.asarray(w_up)[:L].reshape(L, 8, 128, 16, 256).transpose(0, 3, 2, 1, 4)).reshape(L, 16, 128, 8 * 256)
    wdn = np.ascontiguousarray(np.asarray(w_down)[:L].reshape(L, 16, 2, 128, 1024).transpose(0, 1, 3, 2, 4)).reshape(L, 16, 128, 2 * 1024)
    return wqkv, wgate, wout, wo, wup, wdn


def _gains(norm_mix, norm_mlp, norm_final):
    g = np.zeros((128, DEPTH * 16 + 8 + 128), np.float32)
    g[:, DEPTH * 16 + 8:] = 1.0 / D
    nm = np.asarray(norm_mix).reshape(DEPTH, 8, 128)
    nl = np.asarray(norm_mlp).reshape(DEPTH, 8, 128)
    for l in range(DEPTH):
        g[:, l * 16:l * 16 + 8] = nm[l].T
        g[:, l * 16 + 8:l * 16 + 16] = nl[l].T
    g[:, DEPTH * 16:DEPTH * 16 + 8] = np.asarray(norm_final).reshape(8, 128).T
    return g


def make_in_maps(x, norm_mix, w_in, w_out_a, w_out_b, w_out_c, w_o, norm_mlp, w_up, w_down, norm_final, L=DEPTH, cores=8):
    cf, cb = _consts()
    wqkv, wgate, wout, wo, wup, wdn = _layout_weights(w_in, w_out_a, w_out_b, w_out_c, w_o, w_up, w_down, L)
    g = _gains(norm_mix, norm_mlp, norm_final)
    x = np.asarray(x)
    ind = (np.arange(S)[None, :] // 256 == np.arange(16)[:, None]).astype(np.float32)
    maps = []
    for b in range(cores):
        maps.append({"xT": np.ascontiguousarray(x[b].T), "wqkv": wqkv, "wgate": wgate, "wout": wout, "wo": wo,
                     "wup": wup, "wdn": wdn, "gains": g, "cf": cf, "cb": cb, "ind": ind})
    return maps


def kernel(x, norm_mix, w_in, w_out_a, w_out_b, w_out_c, w_o, norm_mlp, w_up, w_down, norm_final):
    nc = build(DEPTH)
    maps = make_in_maps(x, norm_mix, w_in, w_out_a, w_out_b, w_out_c, w_o, norm_mlp, w_up, w_down, norm_final)
    res = run_bass_kernel_spmd(nc, maps, core_ids=list(range(8)))
    out = np.stack([np.asarray(r["outT"]).T for r in res.results], axis=0)
    return np.ascontiguousarray(out.astype(np.float32))
```
